# Optimizing a Trainium2 kernel written in Bass

```python
import math
import jax, jax.numpy as jnp
from jax import lax
import numpy as np

D_MODEL = 1024
BATCH = 4
SEQ = 8192
DEPTH = 2
DEC_BATCH = 16
DEC_SEQ = 2048
PAST_LEN = 128

GRID_W = 64
HEAD_DIM = 64
D_FF = 2816
N_BRANCH = 4
BRANCH_W = 512
CONV_W = BRANCH_W
POOL_W = BRANCH_W
POOL_WINDOWS = (2, 4, 8, 16)
POOL_GROUP = POOL_W // 4
GQA_HEADS = 8
GQA_KV = 2
GQA_GROUP = GQA_HEADS // GQA_KV
DIFF_HEADS = 4
DIFF_V = 2 * HEAD_DIM
AXIAL_THETA = 10000.0
ROPE_THETA = 500000.0
ROPE_DIM = HEAD_DIM // 4
Q_BLOCK = 128
EPS = 1e-6
IN_SIZES = (CONV_W, CONV_W, CONV_W, POOL_W,
            GQA_HEADS * HEAD_DIM, GQA_KV * HEAD_DIM, GQA_KV * HEAD_DIM,
            DIFF_HEADS * 2 * HEAD_DIM, DIFF_HEADS * 2 * HEAD_DIM, DIFF_HEADS * DIFF_V,
            N_BRANCH * D_MODEL)
IN_W = sum(IN_SIZES)

kernel_name = "hybrid_gated_conv_pool_gqa_diff_encoder"


def rms_norm(x, g):
    xf = x.astype(jnp.float32)
    y = xf * lax.rsqrt(jnp.mean(xf * xf, axis=-1, keepdims=True) + EPS)
    return (y * g.astype(jnp.float32)).astype(x.dtype)


def swiglu(x, g, w_in, w_out):
    h = rms_norm(x, g) @ w_in
    a, b = jnp.split(h, 2, axis=-1)
    return (jax.nn.silu(a) * b) @ w_out


def rope(x, pos, theta):
    n = x.shape[-1]
    half = n // 2
    freqs = jnp.exp(-math.log(theta) * jnp.arange(half, dtype=jnp.float32) * (2.0 / n))
    ang = pos.astype(jnp.float32)[:, None] * freqs[None, :]
    shp = (ang.shape[0],) + (1,) * (x.ndim - 3) + (half,)
    cos = jnp.cos(ang).reshape(shp)
    sin = jnp.sin(ang).reshape(shp)
    xf = x.astype(jnp.float32)
    x1, x2 = xf[..., :half], xf[..., half:]
    return jnp.concatenate([x1 * cos - x2 * sin, x2 * cos + x1 * sin], axis=-1).astype(x.dtype)


def axial_rope(x, row, col):
    half = HEAD_DIM // 2
    return jnp.concatenate([rope(x[..., :half], row, AXIAL_THETA),
                            rope(x[..., half:], col, AXIAL_THETA)], axis=-1)


def partial_rope(x, pos):
    return jnp.concatenate([rope(x[..., :ROPE_DIM], pos, ROPE_THETA), x[..., ROPE_DIM:]], axis=-1)


def short_conv(xin, b_gate, c_gate, w):
    z = c_gate * xin
    zp = jnp.pad(z, ((0, 0), (1, 1), (0, 0)))
    y = w[0] * zp[:, :-2] + w[1] * zp[:, 1:-1] + w[2] * zp[:, 2:]
    return b_gate * y


def pool_mixer(p, w, scale):
    B, S, _ = p.shape
    t = jnp.arange(S)
    pf = p.astype(jnp.float32)
    cs = jnp.concatenate([jnp.zeros((B, 1, POOL_W), jnp.float32), jnp.cumsum(pf, axis=1)], axis=1)
    outs = []
    for gi, win in enumerate(POOL_WINDOWS):
        lo = jnp.clip(t - win // 2, 0, S)
        hi = jnp.clip(t - win // 2 + win, 0, S)
        c = cs[..., gi * POOL_GROUP:(gi + 1) * POOL_GROUP]
        s = jnp.take(c, hi, axis=1) - jnp.take(c, lo, axis=1)
        cnt = (hi - lo).astype(jnp.float32)[:, None]
        outs.append(s / cnt - pf[..., gi * POOL_GROUP:(gi + 1) * POOL_GROUP])
    m = jnp.stack(outs, axis=2).astype(p.dtype)
    y = jnp.einsum('bsgc,gcd->bsgd', m, w).reshape(B, S, POOL_W)
    return y * scale


def gqa_attention(q, k, v, q_norm, k_norm, row, col):
    B, S, _ = q.shape
    q = axial_rope(rms_norm(q.reshape(B, S, GQA_HEADS, HEAD_DIM), q_norm), row, col)
    k = axial_rope(rms_norm(k.reshape(B, S, GQA_KV, HEAD_DIM), k_norm), row, col)
    v = v.reshape(B, S, GQA_KV, HEAD_DIM)
    nb = S // Q_BLOCK
    qb = q.reshape(B, nb, Q_BLOCK, GQA_KV, GQA_GROUP, HEAD_DIM).transpose(1, 0, 2, 3, 4, 5)
    scale = HEAD_DIM ** -0.5

    def block(qi):
        s = jnp.einsum('bqkgd,bskd->bkgqs', qi, k).astype(jnp.float32) * scale
        pr = jax.nn.softmax(s, axis=-1).astype(v.dtype)
        return jnp.einsum('bkgqs,bskd->bqkgd', pr, v)

    o = lax.map(block, qb)
    return o.transpose(1, 0, 2, 3, 4, 5).reshape(B, S, GQA_HEADS * HEAD_DIM)


def diff_attention(q, k, v, q_norm, k_norm, lam_vec, out_norm, lambda_init, pos):
    B, S, _ = q.shape
    q = partial_rope(rms_norm(q.reshape(B, S, DIFF_HEADS, 2, HEAD_DIM), q_norm), pos)
    k = partial_rope(rms_norm(k.reshape(B, S, DIFF_HEADS, 2, HEAD_DIM), k_norm), pos)
    v = v.reshape(B, S, DIFF_HEADS, DIFF_V)
    lv = lam_vec.astype(jnp.float32)
    lam = jnp.exp(jnp.sum(lv[0] * lv[1])) - jnp.exp(jnp.sum(lv[2] * lv[3])) + lambda_init
    nb = S // Q_BLOCK
    qb = q.reshape(B, nb, Q_BLOCK, DIFF_HEADS, 2, HEAD_DIM).transpose(1, 0, 2, 3, 4, 5)
    scale = HEAD_DIM ** -0.5

    def block(qi):
        s = jnp.einsum('bqhcd,bshcd->bhcqs', qi, k).astype(jnp.float32) * scale
        pr = jax.nn.softmax(s, axis=-1)
        a = (pr[:, :, 0] - lam * pr[:, :, 1]).astype(v.dtype)
        return jnp.einsum('bhqs,bshe->bqhe', a, v)

    o = lax.map(block, qb)
    o = o.transpose(1, 0, 2, 3, 4).reshape(B, S, DIFF_HEADS, DIFF_V)
    o = rms_norm(o, out_norm) * (1.0 - lambda_init)
    return o.reshape(B, S, DIFF_HEADS * DIFF_V)


def mixer(h, layer_idx, mix_norm, w_in, b_gate, conv_w, pool_w, pool_scale,
          attn_q_norm, attn_k_norm, diff_q_norm, diff_k_norm, diff_lambda, diff_out_norm,
          w_branch, w_out, row, col, pos):
    B, S, _ = h.shape
    u = rms_norm(h, mix_norm)
    z = u @ w_in
    offs = np.cumsum(IN_SIZES)[:-1].tolist()
    a_x, a_b, a_c, p_in, cq, ck, cv, dq, dk, dv, g = jnp.split(z, offs, axis=-1)
    lambda_init = 0.8 - 0.6 * math.exp(-0.3 * layer_idx)
    branches = (
        short_conv(a_x, a_b, a_c, conv_w),
        pool_mixer(p_in, pool_w, pool_scale),
        gqa_attention(cq, ck, cv, attn_q_norm, attn_k_norm, row, col),
        diff_attention(dq, dk, dv, diff_q_norm, diff_k_norm, diff_lambda, diff_out_norm, lambda_init, pos),
    )
    gates = jax.nn.sigmoid(g + b_gate).reshape(B, S, N_BRANCH, D_MODEL)
    merged = gates[:, :, 0] * (branches[0] @ w_branch[0])
    for n in range(1, N_BRANCH):
        merged = merged + gates[:, :, n] * (branches[n] @ w_branch[n])
    return merged @ w_out


def trunk(x, ffn1_norm, ffn1_w_in, ffn1_w_out, mix_norm, w_in, b_gate, conv_w, pool_w, pool_scale,
          attn_q_norm, attn_k_norm, diff_q_norm, diff_k_norm, diff_lambda, diff_out_norm,
          w_branch, w_out, ffn2_norm, ffn2_w_in, ffn2_w_out):
    S = x.shape[1]
    rows = S // GRID_W
    row = jnp.repeat(jnp.arange(rows), GRID_W)
    col = jnp.tile(jnp.arange(GRID_W), rows)
    pos = jnp.arange(S)
    for i in range(DEPTH):
        x = x + 0.5 * swiglu(x, ffn1_norm[i], ffn1_w_in[i], ffn1_w_out[i])
        x = x + mixer(x, i, mix_norm[i], w_in[i], b_gate[i], conv_w[i], pool_w[i], pool_scale[i],
                      attn_q_norm[i], attn_k_norm[i], diff_q_norm[i], diff_k_norm[i],
                      diff_lambda[i], diff_out_norm[i], w_branch[i], w_out[i], row, col, pos)
        x = x + 0.5 * swiglu(x, ffn2_norm[i], ffn2_w_in[i], ffn2_w_out[i])
    return x


def setup_inputs(seed: int = 0) -> dict:
    key = jax.random.key(seed)
    ks = jax.random.split(key, 24)
    f32 = jnp.float32

    def nrm(k, shape, scale):
        return jax.random.normal(k, shape, f32) * scale

    def gain(k, shape):
        return 1.0 + 0.05 * jax.random.normal(k, shape, f32)

    L, D, F = DEPTH, D_MODEL, D_FF
    return {
        "x_prompt": nrm(ks[0], (BATCH, SEQ, D), 1.0),
        "x_sample": nrm(ks[1], (DEC_BATCH, DEC_SEQ, D), 1.0),
        "ffn1_norm": gain(ks[2], (L, D)),
        "ffn1_w_in": nrm(ks[3], (L, D, 2 * F), D ** -0.5),
        "ffn1_w_out": nrm(ks[4], (L, F, D), F ** -0.5),
        "mix_norm": gain(ks[5], (L, D)),
        "w_in": nrm(ks[6], (L, D, IN_W), D ** -0.5),
        "b_gate": nrm(ks[7], (L, N_BRANCH * D), 0.01),
        "conv_w": nrm(ks[8], (L, 3, CONV_W), 3 ** -0.5),
        "pool_w": nrm(ks[9], (L, 4, POOL_GROUP, POOL_GROUP), POOL_GROUP ** -0.5),
        "pool_scale": 0.5 + 0.05 * jax.random.normal(ks[10], (L, POOL_W), f32),
        "attn_q_norm": gain(ks[11], (L, HEAD_DIM)),
        "attn_k_norm": gain(ks[12], (L, HEAD_DIM)),
        "diff_q_norm": gain(ks[13], (L, HEAD_DIM)),
        "diff_k_norm": gain(ks[14], (L, HEAD_DIM)),
        "diff_lambda": nrm(ks[15], (L, 4, HEAD_DIM), 0.1),
        "diff_out_norm": gain(ks[16], (L, DIFF_V)),
        "w_branch": nrm(ks[17], (L, N_BRANCH, BRANCH_W, D), BRANCH_W ** -0.5),
        "w_out": nrm(ks[18], (L, D, D), D ** -0.5),
        "ffn2_norm": gain(ks[19], (L, D)),
        "ffn2_w_in": nrm(ks[20], (L, D, 2 * F), D ** -0.5),
        "ffn2_w_out": nrm(ks[21], (L, F, D), F ** -0.5),
    }


def reference(x_prompt, x_sample, ffn1_norm, ffn1_w_in, ffn1_w_out, mix_norm, w_in, b_gate,
              conv_w, pool_w, pool_scale, attn_q_norm, attn_k_norm, diff_q_norm, diff_k_norm,
              diff_lambda, diff_out_norm, w_branch, w_out, ffn2_norm, ffn2_w_in, ffn2_w_out):
    y_prompt = trunk(x_prompt, ffn1_norm, ffn1_w_in, ffn1_w_out, mix_norm, w_in, b_gate, conv_w,
                     pool_w, pool_scale, attn_q_norm, attn_k_norm, diff_q_norm, diff_k_norm,
                     diff_lambda, diff_out_norm, w_branch, w_out, ffn2_norm, ffn2_w_in, ffn2_w_out)
    y_sample = trunk(x_sample, ffn1_norm, ffn1_w_in, ffn1_w_out, mix_norm, w_in, b_gate, conv_w,
                     pool_w, pool_scale, attn_q_norm, attn_k_norm, diff_q_norm, diff_k_norm,
                     diff_lambda, diff_out_norm, w_branch, w_out, ffn2_norm, ffn2_w_in, ffn2_w_out)
    return (y_prompt, y_sample)
```

```python
import math
import os
from contextlib import ExitStack

import numpy as np
import concourse.bass as bass
import concourse.mybir as mybir
from concourse.bass_utils import run_bass_kernel_spmd

F32 = mybir.dt.float32
BF16 = mybir.dt.bfloat16
AF = mybir.ActivationFunctionType
ALU = mybir.AluOpType

D = 1024
DFF = 2816
INW = 8448
T = 512
EPS = 1e-6
NSLOT = 4
SLOTW = 8 * 512
NPP = 88
NEG = -30000.0


class Dep:
    __slots__ = ("w", "r", "dsem", "accum")

    def __init__(self, accum=False):
        self.w = {}
        self.r = {}
        self.dsem = {}
        self.accum = accum


class TV:
    __slots__ = ("ap", "dep", "sbuf")

    def __init__(self, ap, dep=None, sbuf=True):
        self.ap = ap
        self.dep = dep if dep is not None else Dep()
        self.sbuf = sbuf

    def __getitem__(self, k):
        return TV(self.ap[k], self.dep, self.sbuf)

    def re(self, s, **kw):
        return TV(self.ap.rearrange(s, **kw), self.dep, self.sbuf)


class Bld:
    ENG = ("pe", "act", "dve", "pool", "sp")

    def __init__(self, nc, es, ndsem=90):
        self.nc = nc
        self.ops = {k: [] for k in self.ENG}
        self.cnt = {k: 0 for k in ("pe", "act", "dve", "pool")}
        self.seen = {k: {} for k in self.ENG}
        self.esem = {k: es.enter_context(nc.semaphore("s_" + k)) for k in self.cnt}
        self.free = [es.enter_context(nc.semaphore("d%d" % i)) for i in range(ndsem)]
        self.dsems = []

    def new_dsem(self):
        so = self.free.pop()
        self.dsems.append([so, 0])
        return len(self.dsems) - 1

    def _wait(self, eng, key, val):
        if not isinstance(key, str):
            val = self.dsems[key][1]
        if val <= 0 or self.seen[eng].get(key, 0) >= val:
            return
        self.seen[eng][key] = val
        so = self.esem[key] if isinstance(key, str) else self.dsems[key][0]
        self.ops[eng].append(lambda e, so=so, val=val: e.wait_ge(so, val))

    def issue(self, eng, fn, reads, writes, sig=True):
        sig = True
        idx = self.cnt[eng] + 1
        for tv in reads:
            for k, v in tv.dep.w.items():
                if k == eng:
                    if eng != "pe" and v > self.cnt[eng] - 2:
                        self._wait(eng, k, v)
                else:
                    self._wait(eng, k, v)
        for tv in writes:
            d = tv.dep
            for k, v in d.w.items():
                if k != eng:
                    self._wait(eng, k, v)
            for k, v in d.r.items():
                if k != eng:
                    self._wait(eng, k, v)
        for tv in reads:
            tv.dep.r[eng] = idx
        for tv in writes:
            tv.dep.w = {eng: idx}
            tv.dep.r = {}
        if sig:
            sem = self.esem[eng]
            self.ops[eng].append(lambda e, fn=fn, sem=sem: fn(e).then_inc(sem, 1))
            self.cnt[eng] = idx
        else:
            self.ops[eng].append(fn)

    def dma(self, q, out, in_, key=None):
        sb = out if out.sbuf else (in_ if in_.sbuf else None)
        if key is None:
            if q not in sb.dep.dsem:
                sb.dep.dsem[q] = self.new_dsem()
            key = sb.dep.dsem[q]
        for k, v in in_.dep.w.items():
            self._wait(q, k, v)
        if not out.dep.accum:
            for k, v in out.dep.w.items():
                if k != key:
                    self._wait(q, k, v)
        for k, v in out.dep.r.items():
            self._wait(q, k, v)
        self.dsems[key][1] += 16
        val = self.dsems[key][1]
        so = self.dsems[key][0]
        oa, ia = out.ap, in_.ap
        self.ops[q].append(lambda e: e.dma_start(out=oa, in_=ia).then_inc(so, 16))
        if not in_.dep.accum:
            in_.dep.r[key] = val
        if out.dep.accum:
            out.dep.w[key] = val
        else:
            out.dep.w = {key: val}
            out.dep.r = {}

    def barrier(self):
        for eng in self.ENG:
            for k in self.cnt:
                if k != eng:
                    self._wait(eng, k, self.cnt[k])
            for i, (so, v) in enumerate(self.dsems):
                self._wait(eng, i, v)

    def mm(self, out, lhsT, rhs, start, stop, sig=None):
        sig = stop if sig is None else sig
        o, l, r = out.ap, lhsT.ap, rhs.ap
        self.issue("pe", lambda e: e.matmul(o, l, r, start=start, stop=stop), [lhsT, rhs], [out], sig)

    def tr(self, out, in_, ident, sig=True):
        o, i, d = out.ap, in_.ap, ident.ap
        self.issue("pe", lambda e: e.transpose(o, i, d), [in_, ident], [out], sig)

    def act(self, out, in_, func, bias=None, scale=None):
        reads = [in_]
        kw = {}
        if bias is not None:
            if isinstance(bias, TV):
                reads.append(bias)
                kw["bias"] = bias.ap
            else:
                kw["bias"] = float(bias)
        if scale is not None:
            if isinstance(scale, TV):
                reads.append(scale)
                kw["scale"] = scale.ap
            else:
                kw["scale"] = float(scale)
        o, i = out.ap, in_.ap
        self.issue("act", lambda e: e.activation(out=o, in_=i, func=func, **kw), reads, [out])

    def tt(self, out, in0, in1, op, eng="dve"):
        o, a, b = out.ap, in0.ap, in1.ap
        self.issue(eng, lambda e: e.tensor_tensor(out=o, in0=a, in1=b, op=op), [in0, in1], [out])

    def ts(self, out, in0, s1, op0, s2=None, op1=None, eng="dve"):
        reads = [in0]
        a1 = s1
        if isinstance(s1, TV):
            reads.append(s1)
            a1 = s1.ap
        a2 = s2
        if isinstance(s2, TV):
            reads.append(s2)
            a2 = s2.ap
        o, a = out.ap, in0.ap
        if op1 is None:
            self.issue(eng, lambda e: e.tensor_scalar(o, a, a1, None, op0), reads, [out])
        else:
            self.issue(eng, lambda e: e.tensor_scalar(o, a, a1, a2, op0, op1), reads, [out])

    def stt(self, out, in0, scalar, in1, op0, op1):
        reads = [in0, in1]
        sc = scalar
        if isinstance(scalar, TV):
            reads.append(scalar)
            sc = scalar.ap
        o, a, b = out.ap, in0.ap, in1.ap
        self.issue("dve", lambda e: e.scalar_tensor_tensor(out=o, in0=a, scalar=sc, in1=b, op0=op0, op1=op1),
                   reads, [out])

    def recip(self, out, in_):
        o, i = out.ap, in_.ap
        self.issue("dve", lambda e: e.reciprocal(out=o, in_=i), [in_], [out])

    def copy(self, out, in_, eng="dve"):
        o, i = out.ap, in_.ap
        self.issue(eng, lambda e: e.tensor_copy(out=o, in_=i), [in_], [out])

    def memset(self, out, val, eng="dve"):
        o = out.ap
        self.issue(eng, lambda e: e.memset(o, val), [], [out])


def build_program(NTOK, SSEQ, debug=False, stop=None):
    NT = NTOK // T
    NB = NTOK // 128
    NCH = NTOK // SSEQ
    nc = bass.Bass("TRN2", target_bir_lowering=False)
    es = ExitStack()

    def din(name, shape, dt=F32):
        return TV(nc.dram_tensor(name, list(shape), dt, kind="ExternalInput").ap(), Dep(accum=True), sbuf=False)

    def dscr(name, shape, dt):
        kind = "ExternalOutput" if debug else "Internal"
        return TV(nc.dram_tensor(name, list(shape), dt, kind=kind).ap(), Dep(accum=True), sbuf=False)

    x_in = din("x", [NTOK, D])
    y_out = TV(nc.dram_tensor("y", [NTOK, D], F32, kind="ExternalOutput").ap(), Dep(accum=True), sbuf=False)
    W = {}
    for nm, shp in (("ffn1_w_in", [2, D, 2 * DFF]), ("ffn1_w_out", [2, DFF, D]), ("w_in", [2, D, INW]),
                    ("pool_w", [2, 4, 128, 128]), ("w_branch", [2, 4, 512, D]), ("w_out", [2, D, D]),
                    ("ffn2_w_in", [2, D, 2 * DFF]), ("ffn2_w_out", [2, DFF, D])):
        W[nm] = din(nm, shp)
    pp_in = din("pp", [2, 128, NPP])
    cst_in = din("cst", [5, 128, 128])
    rope_in = din("rope", [4, 64, NTOK])
    mb_in = din("mb", [128, NCH * NCH])
    hf_in = din("hf", [128, 2 * NT])
    icnt_in = din("icnt", [4, NTOK])

    WS = []
    for l in range(2):
        WS.append(dict(
            f1i=dscr("ws_f1i%d" % l, [128, 11, 8, 512], BF16), f1o=dscr("ws_f1o%d" % l, [128, 8, 22, 128], BF16),
            f2i=dscr("ws_f2i%d" % l, [128, 11, 8, 512], BF16), f2o=dscr("ws_f2o%d" % l, [128, 8, 22, 128], BF16),
            win=dscr("ws_in%d" % l, [128, 17, 8, 512], BF16), br=dscr("ws_br%d" % l, [128, 4, 4, 1024], BF16),
            out=dscr("ws_out%d" % l, [128, 2, 8, 512], BF16)))
    SC = []
    for l in range(2):
        SC.append(dict(
            X1=dscr("X1_%d" % l, [8, 128, NTOK], F32), XS=dscr("XS_%d" % l, [8, 128, NTOK], F32),
            CZ=dscr("CZ_%d" % l, [4, 128, NTOK], F32), PIN=dscr("PIN_%d" % l, [4, 128, NTOK], F32),
            QG=dscr("QG_%d" % l, [4, 128, NTOK], BF16), KG=dscr("KG_%d" % l, [128, NTOK], BF16),
            VG=dscr("VG_%d" % l, [128, 2, NB, 128], BF16),
            QD=dscr("QD_%d" % l, [4, 128, NTOK], BF16), KD=dscr("KD_%d" % l, [4, 128, NTOK], BF16),
            VD=dscr("VD_%d" % l, [128, 4, NB, 128], BF16),
            OG=dscr("OG_%d" % l, [4, 128, NTOK], BF16), OD=dscr("OD_%d" % l, [4, 128, NTOK], BF16)))

    AW = 206 * 256
    arena = es.enter_context(nc.sbuf_tensor("arena", [128, AW], F32))
    psall = es.enter_context(nc.psum_tensor("psall", [128, 4096], F32))
    psb = [TV(psall[:, i * 512:(i + 1) * 512], Dep()) for i in range(8)]
    pspair = [TV(psall[:, i * 1024:(i + 1) * 1024], Dep()) for i in range(2)]
    state = {"off": 0, "bank": 0}

    def alloc(nbytes):
        o = state["off"]
        state["off"] = o + (nbytes + 3) // 4
        assert state["off"] <= AW, ("arena overflow", state["off"] * 4)
        return o

    def f32t(n):
        o = alloc(4 * n)
        return arena[:, o:o + n]

    def bf16t(n):
        o = alloc(2 * n)
        return arena[:, o:o + (n + 1) // 2].bitcast(BF16)

    def group(tvs):
        g = {}
        for tv in tvs:
            tv.dep.dsem = g
        return tvs

    def chunks(ap3, n):
        return group([TV(ap3[:, c, :], Dep()) for c in range(n)])

    b = Bld(nc, es)

    def bank():
        k = state["bank"]
        state["bank"] = (k + 1) % 8
        return psb[k]

    ident = TV(f32t(128))
    cbf = bf16t(4 * 128).rearrange("p (a c) -> p a c", a=4)
    ones_t, bd_t, rtax_t, rtpr_t = [TV(cbf[:, i, :]) for i in range(4)]
    pp = [TV(f32t(NPP)) for _ in range(2)]
    poolw = [TV(bf16t(512).rearrange("p (g d) -> p g d", g=4)) for _ in range(2)]
    mb = TV(f32t(NCH * NCH))
    hf = TV(f32t(2 * NT))
    lamt = [TV(f32t(8)) for _ in range(2)]
    ognorm = [TV(f32t(1)) for _ in range(2)]
    group([ident, ones_t, bd_t, rtax_t, rtpr_t, mb, hf] + pp + poolw)
    ring = [TV(bf16t(SLOTW)) for _ in range(NSLOT)]
    wstate = {"i": 0}
    base_off = state["off"]

    def getw(src, kc, cb):
        s = ring[wstate["i"] % NSLOT]
        wstate["i"] += 1
        v = TV(s.ap[:, 0:kc * cb].rearrange("p (k c) -> p k c", k=kc), s.dep)
        b.dma("sp", v, src)
        return v

    C_F1N, C_MXN, C_F2N, C_BG, C_CW, C_PS, C_AQ, C_AK, C_DQ, C_DK, C_DO, C_LV = 0, 8, 16, 24, 56, 68, 72, 73, 74, 75, 76, 77

    b.dma("sp", ident, cst_in[0])
    for i, tvv in enumerate((ones_t, bd_t, rtax_t, rtpr_t)):
        b.dma("pool", tvv, cst_in[1 + i])
    b.dma("sp", mb, mb_in)
    b.dma("sp", hf, hf_in)
    for l in range(2):
        b.dma("sp", pp[l], pp_in[l])
        b.dma("pool", poolw[l], W["pool_w"][l].re("g c d -> c g d"))

    conv_thunks = []

    def convert_layer(l, defer=False):
        ws = WS[l]

        def cv_dma(dst, src, key):
            if defer:
                conv_thunks.append(lambda: b.dma("pool", dst, src, key=key))
            else:
                b.dma("pool", dst, src, key=key)

        def cv_wait(key):
            if defer:
                conv_thunks.append(lambda: b._wait("pool", key, 1))
            else:
                b._wait("pool", key, 1)

        for nm_i, nm_o, wi, wo in (("f1i", "f1o", "ffn1_w_in", "ffn1_w_out"), ("f2i", "f2o", "ffn2_w_in", "ffn2_w_out")):
            if nm_i == "f2i":
                k = b.new_dsem(); pass
                segs = ((0, 1, 0, 512), (1, 4, 1024, 1536), (4, 5, 2560, 256), (5, 8, 2816, 1536), (8, 9, 512, 512),
                        (9, 17, 4352, 4096))
                for kc in range(8):
                    for b0, b1, c0, wd in segs:
                        src = W["w_in"][l, kc * 128:(kc + 1) * 128, c0:c0 + wd]
                        if b1 - b0 == 1:
                            dst = ws["win"][:, b0, kc, 0:wd]
                        else:
                            dst = ws["win"][:, b0:b1, kc, :]
                            src = src.re("p (b c) -> p b c", b=b1 - b0)
                        cv_dma(dst, src, k)
                    if kc % 2 == 1:
                        cv_wait(k)
                k = b.new_dsem(); pass
                for n in range(4):
                    cv_dma(ws["br"][:, n, :, :], W["w_branch"][l, n].re("(k p) c -> p k c", p=128), k)
                k = b.new_dsem(); pass
                for kc in range(8):
                    cv_dma(ws["out"][:, :, kc, :],
                           W["w_out"][l, kc * 128:(kc + 1) * 128, :].re("p (b c) -> p b c", b=2), k)
            k = b.new_dsem(); pass
            for kc in range(8):
                for half in range(2):
                    src = W[wi][l, kc * 128:(kc + 1) * 128, half * DFF:(half + 1) * DFF].re("p (b c) -> p b c", b=11)
                    cv_dma(ws[nm_i][:, :, kc, half * 256:(half + 1) * 256], src, k)
                if kc % 4 == 3:
                    cv_wait(k)
            k = b.new_dsem(); pass
            for kc in range(22):
                src = W[wo][l, kc * 128:(kc + 1) * 128, :].re("p (b c) -> p b c", b=8)
                cv_dma(ws[nm_o][:, :, kc, :], src, k)
                if kc % 8 == 7:
                    cv_wait(k)

    convert_layer(0)

    def lam_setup(l):
        lambda_init = 0.8 - 0.6 * math.exp(-0.3 * l)
        prod = TV(lamt[l].ap[:, 2:4], lamt[l].dep)
        b.tt(prod, pp[l][:, C_LV:C_LV + 4:2], pp[l][:, C_LV + 1:C_LV + 4:2], ALU.mult)
        pb = bank()
        b.mm(TV(pb.ap[:, 0:2], pb.dep), ident_ones, prod, True, True)
        ex = TV(lamt[l].ap[:, 4:6], lamt[l].dep)
        b.act(ex, TV(pb.ap[:, 0:2], pb.dep), AF.Exp)
        d = TV(lamt[l].ap[:, 6:7], lamt[l].dep)
        b.tt(d, ex[:, 1:2], ex[:, 0:1], ALU.subtract)
        b.ts(TV(lamt[l].ap[:, 0:1], lamt[l].dep), d, -lambda_init, ALU.add)
        b.ts(ognorm[l], pp[l][:, C_DO:C_DO + 1], 1.0 - lambda_init, ALU.mult)

    ones32 = TV(f32t(128))
    ident_ones = ones32
    b.memset(ones32, 1.0)
    epst = TV(f32t(1))
    b.memset(epst, EPS)
    base_off = state["off"]
    for l in range(2):
        lam_setup(l)

    def rmsnorm(xT, gcol, l, uT, sq, tmpa, tmpb):
        ssb = bank()
        for c in range(8):
            b.act(sq[c % 2], xT[c], AF.Square)
            b.mm(ssb, ones_t, sq[c % 2], c == 0, c == 7)
        b.act(tmpa, ssb, AF.Sqrt, bias=EPS, scale=1.0 / D)
        b.recip(tmpb, tmpa)
        for c in range(8):
            b.stt(uT[c], xT[c], pp[l][:, gcol + c:gcol + c + 1], tmpb, ALU.mult, ALU.mult)

    def ffn(xT, uT, hT, wsi, wso, silu_t):
        for j in range(11):
            wb = getw(wsi[:, j, :, :], 8, 512)
            for hh in range(2):
                hc = 2 * j + hh
                pa = bank()
                for kc in range(8):
                    b.mm(pa, wb[:, kc, hh * 128:(hh + 1) * 128], uT[kc], kc == 0, kc == 7)
                pbk = bank()
                for kc in range(8):
                    b.mm(pbk, wb[:, kc, 256 + hh * 128:256 + (hh + 1) * 128], uT[kc], kc == 0, kc == 7)
                st = silu_t[hc % 2]
                b.act(st, pa, AF.Silu)
                b.tt(hT[hc], st, pbk, ALU.mult)
        for oc in range(8):
            wb = getw(wso[:, oc, :, :], 22, 128)
            py = bank()
            for hc in range(22):
                b.mm(py, wb[:, hc, :], hT[hc], hc == 0, hc == 21)
            b.stt(xT[oc], py, 0.5, xT[oc], ALU.mult, ALU.add)

    state["off"] = base_off
    xTb = [chunks(f32t(8 * T).rearrange("p (c t) -> p c t", c=8), 8) for _ in range(2)]
    xTall = None
    uT = chunks(bf16t(8 * T).rearrange("p (c t) -> p c t", c=8), 8)
    hT = chunks(bf16t(22 * T).rearrange("p (c t) -> p c t", c=22), 22)
    sq = [TV(bf16t(T)) for _ in range(2)]
    tmp = [TV(f32t(T)) for _ in range(8)]
    shared_off = state["off"]

    xtok = TV(f32t(4 * D).rearrange("p (a d) -> p a d", a=4))
    ropet2 = [group([TV(f32t(T)) for _ in range(4)]) for _ in range(2)]
    qk_st_ap = bf16t(13 * T).rearrange("p (c t) -> p c t", c=13)
    qk_st = chunks(qk_st_ap, 13)
    vg_st = TV(bf16t(2 * 4 * 128).rearrange("p (k a d) -> p k a d", k=2, a=4))
    vd_st = TV(bf16t(4 * 4 * 128).rearrange("p (h a d) -> p h a d", h=4, a=4))
    cz_st_ap = f32t(4 * T).rearrange("p (c t) -> p c t", c=4)
    cz_st = chunks(cz_st_ap, 4)
    pin_st_ap = f32t(4 * T).rearrange("p (c t) -> p c t", c=4)
    pin_st = chunks(pin_st_ap, 4)
    a_end = state["off"]

    state["off"] = shared_off
    xtok_c = [TV(f32t(D)) for _ in range(2)]
    merged = chunks(f32t(8 * T).rearrange("p (c t) -> p c t", c=8), 8)
    merged_bf = chunks(bf16t(8 * T).rearrange("p (c t) -> p c t", c=8), 8)
    czh_ap = f32t(4 * (T + 2)).rearrange("p (c t) -> p c t", c=4)
    czh = TV(czh_ap)
    pinh_ap = f32t(4 * (T + 16)).rearrange("p (c t) -> p c t", c=4)
    pinh = TV(pinh_ap)
    icnt_t = TV(f32t(4 * T).rearrange("p (c t) -> p c t", c=4))
    o_ld_ap = bf16t(8 * T).rearrange("p (c t) -> p c t", c=8)
    o_ld = TV(o_ld_ap)
    yA = chunks(bf16t(4 * T).rearrange("p (c t) -> p c t", c=4), 4)
    yB = chunks(bf16t(4 * T).rearrange("p (c t) -> p c t", c=4), 4)
    dw = [TV(f32t(T + 16)) for _ in range(3)]
    m_bf4 = [TV(bf16t(T)) for _ in range(4)]
    c_end = state["off"]

    state["off"] = base_off
    kT = [TV(bf16t(NTOK)) for _ in range(2)]
    vT = [TV(bf16t(NB * 128).rearrange("p (n d) -> p n d", n=NB)) for _ in range(2)]
    qT = [TV(bf16t(NTOK)) for _ in range(2)]
    pTp = [TV(bf16t(2 * T)) for _ in range(4)]
    sqd = [TV(bf16t(T)) for _ in range(2)]
    btmp = [TV(f32t(T)) for _ in range(6)]
    ost = [TV(bf16t(T)) for _ in range(2)]
    b_end = state["off"]
    assert max(a_end, c_end, b_end) <= AW

    def qknorm_jobs(ps, gcol, l, ax, out, ropet):
        cosT, sinT = (ropet[0], ropet[1]) if ax else (ropet[2], ropet[3])
        rt = rtax_t if ax else rtpr_t
        st = {}

        def s1():
            st["sq"] = sq[state["bank"] % 2]
            b.act(st["sq"], ps, AF.Square)
            st["ss"] = bank()
            b.mm(st["ss"], bd_t, st["sq"], True, True)

        def s2():
            k = st["k"]
            b.act(tmp[k], st["ss"], AF.Sqrt, bias=EPS, scale=1.0 / 64)
            b.recip(tmp[k + 1], tmp[k])
            b.stt(tmp[k], ps, pp[l][:, gcol:gcol + 1], tmp[k + 1], ALU.mult, ALU.mult)
            st["qb"] = sq[(state["bank"] + 1) % 2]
            b.act(st["qb"], tmp[k], AF.Copy)
            st["rot"] = bank()
            b.mm(st["rot"], rt, st["qb"], True, True)

        def s3():
            k = st["k"]
            b.tt(tmp[k], tmp[k], cosT, ALU.mult, eng="pool")
            b.tt(tmp[k + 1], st["rot"], sinT, ALU.mult)
            b.tt(out, tmp[k], tmp[k + 1], ALU.add, eng="pool")

        return st, [s1, s2, s3]

    SUB = int(os.environ.get("KSUB", "99"))

    def phase_A(l):
        ws, sc = WS[l], SC[l]
        b.memset(TV(vg_st.ap[:, :, :, 64:128], vg_st.dep), 1.0)
        def loads_A(t):
            tok = slice(t * T, (t + 1) * T)
            if l == 0:
                b.dma("sp", xtok, x_in[tok, :].re("(a p) d -> p a d", p=128))
            else:
                for c in range(8):
                    b.dma("sp", xTb[t % 2][c], SC[l - 1]["XS"][c, :, tok])
            for i in range(4):
                r = ropet2[t % 2][i]
                b.dma("sp", r[0:64, :], rope_in[i, :, tok])
                b.dma("sp", r[64:128, :], rope_in[i, :, tok])

        loads_A(0)
        for t in range(NT):
            xT = xTb[t % 2]
            ropet = ropet2[t % 2]
            tok = slice(t * T, (t + 1) * T)
            if l == 0:
                for c in range(8):
                    pb = bank()
                    for a in range(4):
                        b.tr(TV(pb.ap[:, a * 128:(a + 1) * 128], pb.dep), xtok[:, a, c * 128:(c + 1) * 128], ident,
                             sig=(a == 3))
                    if c % 2 == 0:
                        b.act(xT[c], pb, AF.Copy)
                    else:
                        b.copy(xT[c], pb)
            if SUB <= 0:
                return
            rmsnorm(xT, C_F1N, l, uT, sq, tmp[0], tmp[1])
            if SUB <= 1:
                return
            ffn(xT, uT, hT, ws["f1i"], ws["f1o"], tmp[2:4])
            if t + 1 < NT:
                loads_A(t + 1)
            for c in range(8):
                b.dma("pool", sc["X1"][c, :, tok], xT[c])
            rmsnorm(xT, C_MXN, l, uT, sq, tmp[0], tmp[1])
            if SUB <= 3:
                return

            def proj(wb, col0):
                pb = bank()
                for kc in range(8):
                    b.mm(pb, wb[:, kc, col0:col0 + 128], uT[kc], kc == 0, kc == 7)
                return pb

            wb = getw(ws["win"][:, 0, :, :], 8, 512)
            for c in range(4):
                pb = proj(wb, c * 128)
                b.act(cz_st[c], pb, AF.Copy)
            wb = getw(ws["win"][:, 1, :, :], 8, 512)
            for c in range(4):
                pb = proj(wb, c * 128)
                b.tt(cz_st[c], cz_st[c], pb, ALU.mult)
                b.dma("pool", sc["CZ"][c, :, tok], cz_st[c])
            wb = getw(ws["win"][:, 2, :, :], 8, 512)
            for c in range(4):
                pb = proj(wb, c * 128)
                if c % 2 == 0:
                    b.act(pin_st[c], pb, AF.Copy)
                else:
                    b.copy(pin_st[c], pb)
                b.dma("pool", sc["PIN"][c, :, tok], pin_st[c])
            if SUB <= 4:
                return
            pend = []

            def pump(newjob=None):
                if newjob is not None:
                    pend.append([newjob[0], list(newjob[1])])
                for jb in list(pend)[::-1]:
                    pass
                for jb in list(pend):
                    jb[1].pop(0)()
                    if not jb[1]:
                        pend.remove(jb)

            kslot = {"i": 0}

            def qk_chunk(wb, col0, gcol, ax, out):
                pb = proj(wb, col0)
                stt_, stages = qknorm_jobs(pb, gcol, l, ax, out, ropet)
                stt_["k"] = 2 + 2 * (kslot["i"] % 3)
                kslot["i"] += 1
                pump((stt_, stages))

            wb = getw(ws["win"][:, 3, :, :], 8, 512)
            for c in range(4):
                qk_chunk(wb, c * 128, C_AQ, True, qk_st[c])
            wb = getw(ws["win"][:, 4, :, :], 8, 512)
            qk_chunk(wb, 0, C_AK, True, qk_st[4])
            pbv = bank()
            for a in range(4):
                for kc in range(8):
                    b.mm(TV(pbv.ap[:, a * 128:(a + 1) * 128], pbv.dep), uT[kc][:, a * 128:(a + 1) * 128],
                         wb[:, kc, 128:256], kc == 0, kc == 7, sig=(kc == 7 and a == 3))
            b.act(TV(vg_st.ap[:, :, :, 0:64], vg_st.dep),
                  TV(pbv.ap.rearrange("p (a k d) -> p k a d", a=4, k=2), pbv.dep), AF.Copy)
            wb = getw(ws["win"][:, 5, :, :], 8, 512)
            for c in range(4):
                qk_chunk(wb, c * 128, C_DQ, False, qk_st[5 + c])
            wb = getw(ws["win"][:, 6, :, :], 8, 512)
            for c in range(4):
                qk_chunk(wb, c * 128, C_DK, False, qk_st[9 + c])
            wb = getw(ws["win"][:, 7, :, :], 8, 512)
            for a in range(4):
                pb = bank()
                for kc in range(8):
                    b.mm(pb, uT[kc][:, a * 128:(a + 1) * 128], wb[:, kc, :], kc == 0, kc == 7)
                dst = TV(vd_st.ap[:, :, a, :], vd_st.dep)
                src = TV(pb.ap.rearrange("p (h d) -> p h d", h=4), pb.dep)
                if a % 2 == 0:
                    b.act(dst, src, AF.Copy)
                else:
                    b.copy(dst, src)
                pump()
            while pend:
                pump()
            if SUB <= 5:
                return
            for c in range(4):
                b.dma("pool", sc["QG"][c, :, tok], qk_st[c])
                b.dma("pool", sc["QD"][c, :, tok], qk_st[5 + c])
                b.dma("pool", sc["KD"][c, :, tok], qk_st[9 + c])
            b.dma("pool", sc["KG"][:, tok], qk_st[4])
            b.dma("pool", sc["VG"][:, :, 4 * t:4 * t + 4, :], vg_st)
            b.dma("pool", sc["VD"][:, :, 4 * t:4 * t + 4, :], vd_st)

    def phase_B(l):
        sc = SC[l]
        scale = 1.0 / 8.0
        SKEW = 3
        s_banks = [psb[0], psb[1], psb[2], psb[3]]
        it = {"s": 0, "p": 0, "g": 0}
        deferred = []

        def run_pipe(items, flush=False):
            n = len(items)
            per = max(1, n // 100)
            for i in range(n + SKEW + 3):
                if conv_thunks and i % per == 0:
                    conv_thunks.pop(0)()
                if i < n:
                    items[i][0]()
                for dl in [d for d in deferred if d[0] <= i]:
                    deferred.remove(dl)
                    dl[1]()
                if SKEW <= i < n + SKEW:
                    items[i - SKEW][1](i)
            assert not deferred
            while flush and conv_thunks:
                conv_thunks.pop(0)()

        items = []
        for j in range(2):
            for g in range(0, 4, 2):
                ch = (4 * j + g) // 2
                for qi in range(NT):
                    for kb in range(NB):
                        def s1(j=j, g=g, ch=ch, qi=qi, kb=kb, st={}):
                            kt, vt = kT[j % 2], vT[j % 2]
                            if g == 0 and qi == 0 and kb == 0:
                                b.dma("sp", kt[0:64, :], sc["KG"][j * 64:(j + 1) * 64, :])
                                b.dma("sp", kt[64:128, :], sc["KG"][j * 64:(j + 1) * 64, :])
                                b.dma("sp", vt, sc["VG"][:, j, :, :])
                            qt_ = qT[ch % 2]
                            if qi == 0 and kb == 0:
                                b.dma("sp", qt_, sc["QG"][ch, :, :])
                            qc = (qi * T) // SSEQ
                            kc_ = (kb * 128) // SSEQ
                            bias = mb[:, qc * NCH + kc_:qc * NCH + kc_ + 1]
                            sp_ = pspair[it["s"] % 2]
                            it["s"] += 1
                            for hh in range(2):
                                pr = slice(hh * 64, (hh + 1) * 64)
                                b.mm(sp_[:, hh * T:(hh + 1) * T], kt[pr, kb * 128:(kb + 1) * 128],
                                     qt_[pr, qi * T:(qi + 1) * T], True, True)
                            pp_ = pTp[it["p"] % 4]
                            it["p"] += 1
                            b.act(pp_, sp_, AF.Exp, bias=bias, scale=scale)
                            st["p"] = [pp_[:, 0:T], pp_[:, T:2 * T]]

                        def s2(i_, j=j, ch=ch, qi=qi, kb=kb, st=s1.__defaults__[-1]):
                            vt = vT[j % 2]
                            if kb == 0:
                                it["g"] += 1
                            gi_ = it["g"]
                            obs = [psb[4 + 2 * (gi_ % 2)], psb[5 + 2 * (gi_ % 2)]]
                            for hh in range(2):
                                b.mm(obs[hh], vt[:, kb, :], st["p"][hh], kb == 0, kb == NB - 1)
                            if kb == NB - 1:
                                for hh in range(2):
                                    pr = slice(hh * 64, (hh + 1) * 64)
                                    rc = btmp[hh]
                                    b.recip(rc[0:64, :], obs[hh][64:128, :])
                                    os_ = ost[hh]
                                    b.tt(os_[0:64, :], obs[hh][0:64, :], rc[0:64, :], ALU.mult)
                                    b.dma("pool", sc["OG"][ch, pr, qi * T:(qi + 1) * T], os_[0:64, :])
                        items.append((s1, s2))
        run_pipe(items)

        acc = [psb[4], psb[5], psb[6], psb[7]]
        items = []
        for h in range(4):
            for qi in range(NT):
                for kb in range(NB):
                    def s1(h=h, qi=qi, kb=kb, st={}):
                        kt, vt, qt_ = kT[h % 2], vT[h % 2], qT[h % 2]
                        if qi == 0 and kb == 0:
                            b.dma("sp", kt, sc["KD"][h, :, :])
                            b.dma("sp", vt, sc["VD"][:, h, :, :])
                            b.dma("sp", qt_, sc["QD"][h, :, :])
                        qc = (qi * T) // SSEQ
                        kc_ = (kb * 128) // SSEQ
                        bias = mb[:, qc * NCH + kc_:qc * NCH + kc_ + 1]
                        sp_ = pspair[it["s"] % 2]
                        it["s"] += 1
                        for cmp_ in range(2):
                            pr = slice(cmp_ * 64, (cmp_ + 1) * 64)
                            b.mm(sp_[:, cmp_ * T:(cmp_ + 1) * T], kt[pr, kb * 128:(kb + 1) * 128],
                                 qt_[pr, qi * T:(qi + 1) * T], True, True)
                        pp_ = pTp[it["p"] % 4]
                        it["p"] += 1
                        b.act(pp_, sp_, AF.Exp, bias=bias, scale=scale)
                        st["p"] = [pp_[:, 0:T], pp_[:, T:2 * T]]

                    def s2(i_, h=h, qi=qi, kb=kb, st=s1.__defaults__[-1]):
                        vt = vT[h % 2]
                        st_, sp_ = kb == 0, kb == NB - 1
                        for cmp_ in range(2):
                            b.mm(acc[cmp_], vt[:, kb, :], st["p"][cmp_], st_, sp_)
                        for cmp_ in range(2):
                            b.mm(acc[2 + cmp_], ones_t, st["p"][cmp_], st_, sp_)
                        if kb == NB - 1:
                            b.recip(btmp[0], acc[2])
                            b.recip(btmp[1], acc[3])
                            b.tt(btmp[0], acc[0], btmp[0], ALU.mult)
                            b.tt(btmp[1], acc[1], btmp[1], ALU.mult)
                            b.stt(btmp[2], btmp[1], lamt[l][:, 0:1], btmp[0], ALU.mult, ALU.add)
                            sqv = sqd[qi % 2]
                            b.act(sqv, btmp[2], AF.Square)

                            def fin2(h=h, qi=qi, sqv=sqv):
                                ssb = pspair[it["s"] % 2][:, 0:T]
                                it["s"] += 1
                                b.mm(ssb, ones_t, sqv, True, True)
                                b.act(btmp[3], ssb, AF.Ln, bias=epst[:, 0:1], scale=1.0 / 128)
                                b.act(btmp[4], btmp[3], AF.Exp, scale=-0.5)
                                os_ = ost[qi % 2]
                                b.stt(os_, btmp[2], ognorm[l], btmp[4], ALU.mult, ALU.mult)
                                b.dma("pool", sc["OD"][h, :, qi * T:(qi + 1) * T], os_)
                            deferred.append((i_ + 3, fin2))
                    items.append((s1, s2))
        run_pipe(items, flush=True)

    def phase_C(l, last):
        ws, sc = WS[l], SC[l]

        def loads_C(t):
            xT = xTb[t % 2]
            tok = slice(t * T, (t + 1) * T)
            for c in range(8):
                b.dma("sp", xT[c], sc["X1"][c, :, tok])
            lo, hi = t * T - 1, (t + 1) * T + 1
            if t == 0:
                b.memset(TV(czh.ap[:, :, 0:1], czh.dep), 0.0)
                b.dma("sp", TV(czh.ap[:, :, 1:T + 2], czh.dep), sc["CZ"][:, :, 0:T + 1].re("c p t -> p c t"))
            elif t == NT - 1:
                b.memset(TV(czh.ap[:, :, T + 1:T + 2], czh.dep), 0.0)
                b.dma("sp", TV(czh.ap[:, :, 0:T + 1], czh.dep), sc["CZ"][:, :, lo:lo + T + 1].re("c p t -> p c t"))
            else:
                b.dma("sp", czh, sc["CZ"][:, :, lo:hi].re("c p t -> p c t"))
            lo, hi = t * T - 8, (t + 1) * T + 8
            if t == 0:
                b.memset(TV(pinh.ap[:, :, 0:8], pinh.dep), 0.0)
                b.dma("sp", TV(pinh.ap[:, :, 8:T + 16], pinh.dep), sc["PIN"][:, :, 0:T + 8].re("c p t -> p c t"))
            elif t == NT - 1:
                b.memset(TV(pinh.ap[:, :, T + 8:T + 16], pinh.dep), 0.0)
                b.dma("sp", TV(pinh.ap[:, :, 0:T + 8], pinh.dep), sc["PIN"][:, :, lo:lo + T + 8].re("c p t -> p c t"))
            else:
                b.dma("sp", pinh, sc["PIN"][:, :, lo:hi].re("c p t -> p c t"))
            for gi in range(4):
                b.dma("sp", TV(icnt_t.ap[:, gi:gi + 1, :], icnt_t.dep),
                      TV(icnt_in.ap[gi:gi + 1, tok].partition_broadcast(128), icnt_in.dep, False))
            b.dma("sp", TV(o_ld.ap[:, 0:4, :], o_ld.dep), sc["OG"][:, :, tok].re("c p t -> p c t"))
            b.dma("sp", TV(o_ld.ap[:, 4:8, :], o_ld.dep), sc["OD"][:, :, tok].re("c p t -> p c t"))

        loads_C(0)
        for t in range(NT):
            xT = xTb[t % 2]
            tok = slice(t * T, (t + 1) * T)
            b.ts(TV(czh.ap[:, :, 0:1], czh.dep), TV(czh.ap[:, :, 0:1], czh.dep), hf[:, 2 * t:2 * t + 1], ALU.mult)
            b.ts(TV(czh.ap[:, :, T + 1:T + 2], czh.dep), TV(czh.ap[:, :, T + 1:T + 2], czh.dep),
                 hf[:, 2 * t + 1:2 * t + 2], ALU.mult)
            b.ts(TV(pinh.ap[:, :, 0:8], pinh.dep), TV(pinh.ap[:, :, 0:8], pinh.dep), hf[:, 2 * t:2 * t + 1], ALU.mult)
            b.ts(TV(pinh.ap[:, :, T + 8:T + 16], pinh.dep), TV(pinh.ap[:, :, T + 8:T + 16], pinh.dep),
                 hf[:, 2 * t + 1:2 * t + 2], ALU.mult)
            rmsnorm(xT, C_MXN, l, uT, sq, tmp[0], tmp[1])

            def proj(wb, col0):
                pb = bank()
                for kc in range(8):
                    b.mm(pb, wb[:, kc, col0:col0 + 128], uT[kc], kc == 0, kc == 7)
                return pb

            for gi in range(4):
                pc = TV(pinh.ap[:, gi, :], pinh.dep)
                win = (2, 4, 8, 16)[gi]
                cur = pc
                n = T + 16
                step = 1
                k = 0
                while step < win:
                    nn = n - step
                    dst = dw[k % 3]
                    b.tt(dst[:, 0:nn], cur[:, 0:nn], cur[:, step:step + nn], ALU.add, eng="pool")
                    cur = dst
                    n = nn
                    step *= 2
                    k += 1
                off = 8 - win // 2
                mt = tmp[4 + (gi % 2)]
                b.tt(mt, cur[:, off:off + T], TV(icnt_t.ap[:, gi, :], icnt_t.dep), ALU.mult)
                b.tt(m_bf4[gi], mt, pc[:, 8:8 + T], ALU.subtract)
            wb = getw(ws["win"][:, 8, :, :], 8, 512)
            for c in range(4):
                pb = proj(wb, c * 128)
                cw = lambda k: pp[l][:, C_CW + k * 4 + c:C_CW + k * 4 + c + 1]
                tm = tmp[2 + (c % 2)]
                zc = TV(czh.ap[:, c, :], czh.dep)
                b.ts(tm, zc[:, 1:T + 1], cw(1), ALU.mult)
                b.stt(tm, zc[:, 0:T], cw(0), tm, ALU.mult, ALU.add)
                b.stt(tm, zc[:, 2:T + 2], cw(2), tm, ALU.mult, ALU.add)
                b.tt(yA[c], tm, pb, ALU.mult)
            for gi in range(4):
                pb = bank()
                b.mm(pb, poolw[l][:, gi, :], m_bf4[gi], True, True)
                b.act(yB[gi], pb, AF.Copy, scale=pp[l][:, C_PS + gi:C_PS + gi + 1])
            for n in range(4):
                if n == 0:
                    yn = yA
                elif n == 1:
                    yn = yB
                else:
                    yn = [TV(o_ld.ap[:, (n - 2) * 4 + kc, :], o_ld.dep) for kc in range(4)]
                wbr = getw(ws["br"][:, n, :, :], 4, 1024)
                for half in range(2):
                    wg = getw(ws["win"][:, 9 + 2 * n + half, :, :], 8, 512)
                    for o4 in range(4):
                        oc = half * 4 + o4
                        pg = proj(wg, o4 * 128)
                        gt = tmp[2 + (oc % 2)]
                        b.act(gt, pg, AF.Sigmoid, bias=pp[l][:, C_BG + n * 8 + oc:C_BG + n * 8 + oc + 1])
                        pbr = bank()
                        for kc in range(4):
                            b.mm(pbr, wbr[:, kc, oc * 128:(oc + 1) * 128], yn[kc], kc == 0, kc == 3)
                        if n == 0:
                            b.tt(merged[oc], gt, pbr, ALU.mult)
                        else:
                            b.tt(gt, gt, pbr, ALU.mult)
                            b.tt(merged_bf[oc] if n == 3 else merged[oc], merged[oc], gt, ALU.add)
            for half in range(2):
                wo = getw(ws["out"][:, half, :, :], 8, 512)
                for o4 in range(4):
                    oc = half * 4 + o4
                    pb = bank()
                    for kc in range(8):
                        b.mm(pb, wo[:, kc, o4 * 128:(o4 + 1) * 128], merged_bf[kc], kc == 0, kc == 7)
                    b.tt(xT[oc], xT[oc], pb, ALU.add)
            if t + 1 < NT:
                loads_C(t + 1)
            rmsnorm(xT, C_F2N, l, uT, sq, tmp[0], tmp[1])
            ffn(xT, uT, hT, ws["f2i"], ws["f2o"], tmp[2:4])
            if last:
                for a in range(4):
                    for half in range(2):
                        pb = bank()
                        for c4 in range(4):
                            c = half * 4 + c4
                            b.tr(TV(pb.ap[:, c4 * 128:(c4 + 1) * 128], pb.dep), xT[c][:, a * 128:(a + 1) * 128], ident,
                                 sig=(c4 == 3))
                        dst = xtok_c[a % 2][:, half * 512:(half + 1) * 512]
                        if half == 0:
                            b.act(dst, pb, AF.Copy)
                        else:
                            b.copy(dst, pb)
                    b.dma("pool", y_out[t * T + a * 128:t * T + (a + 1) * 128, :], xtok_c[a % 2])
            else:
                for c in range(8):
                    b.dma("pool", sc["XS"][c, :, tok], xT[c])

    plan = [("A0", lambda: phase_A(0)), ("cv1", lambda: convert_layer(1, defer=True)), ("B0", lambda: phase_B(0)),
            ("C0", lambda: phase_C(0, False)), ("A1", lambda: phase_A(1)), ("B1", lambda: phase_B(1)),
            ("C1", lambda: phase_C(1, True))]
    b.barrier()
    for nm, fn in plan:
        if stop == "pro":
            break
        for eng in b.ENG:
            b.ops[eng].append(("scope", nm))
        fn()
        if nm != "cv1":
            b.barrier()
        if nm == stop:
            break

    def replay(e, lst):
        cur = None
        for f in lst:
            if isinstance(f, tuple):
                if cur is not None:
                    cur.__exit__(None, None, None)
                cur = nc.named_scope(f[1])
                cur.__enter__()
            else:
                f(e)
        if cur is not None:
            cur.__exit__(None, None, None)

    with nc.Block() as block:
        @block.sync
        def _(e):
            replay(e, b.ops["sp"])

        @block.tensor
        def _(e):
            replay(e, b.ops["pe"])

        @block.scalar
        def _(e):
            replay(e, b.ops["act"])

        @block.vector
        def _(e):
            replay(e, b.ops["dve"])

        @block.gpsimd
        def _(e):
            replay(e, b.ops["pool"])
    es.close()
    return nc


def _rope_tables(pos_in_seq):
    n_tok = pos_in_seq.shape[0]
    t = pos_in_seq
    row = (t // 64).astype(np.float32)
    col = (t % 64).astype(np.float32)
    pos = t.astype(np.float32)
    axc = np.ones((64, n_tok), np.float32)
    axs = np.zeros((64, n_tok), np.float32)
    fr = np.exp(np.float32(-math.log(10000.0)) * np.arange(16, dtype=np.float32) * np.float32(2.0 / 32)).astype(np.float32)
    for d in range(64):
        p = row if d < 32 else col
        i = d % 16
        ang = (p * fr[i]).astype(np.float32)
        axc[d] = np.cos(ang)
        sgn = -1.0 if (d % 32) < 16 else 1.0
        axs[d] = sgn * np.sin(ang)
    prc = np.ones((64, n_tok), np.float32)
    prs = np.zeros((64, n_tok), np.float32)
    fr2 = np.exp(np.float32(-math.log(500000.0)) * np.arange(8, dtype=np.float32) * np.float32(2.0 / 16)).astype(np.float32)
    for d in range(16):
        i = d % 8
        ang = (pos * fr2[i]).astype(np.float32)
        prc[d] = np.cos(ang)
        sgn = -1.0 if d < 8 else 1.0
        prs[d] = sgn * np.sin(ang)
    return np.stack([axc, axs, prc, prs]).astype(np.float32)


def _consts():
    ident = np.eye(128, dtype=np.float32)
    ones = np.ones((128, 128), np.float32)
    bd = np.zeros((128, 128), np.float32)
    bd[0:64, 0:64] = 1.0
    bd[64:128, 64:128] = 1.0
    rtax = np.zeros((128, 128), np.float32)
    rtpr = np.zeros((128, 128), np.float32)
    for m in range(128):
        hb, d = (m // 64) * 64, m % 64
        blk, i = d // 32, d % 32
        partner = hb + blk * 32 + (i + 16) % 32
        rtax[partner, m] = 1.0
        if d < 16:
            partner = hb + (d + 8) % 16
            rtpr[partner, m] = 1.0
    return np.stack([ident, ones, bd, rtax, rtpr]).astype(np.float32)


def _pack_params(inp, l):
    pp = np.zeros((128, NPP), np.float32)
    pp[:, 0:8] = inp["ffn1_norm"][l].reshape(8, 128).T
    pp[:, 8:16] = inp["mix_norm"][l].reshape(8, 128).T
    pp[:, 16:24] = inp["ffn2_norm"][l].reshape(8, 128).T
    pp[:, 24:56] = inp["b_gate"][l].reshape(32, 128).T
    pp[:, 56:68] = inp["conv_w"][l].reshape(3, 4, 128).transpose(2, 0, 1).reshape(128, 12)
    pp[:, 68:72] = inp["pool_scale"][l].reshape(4, 128).T
    pp[:, 72] = np.tile(inp["attn_q_norm"][l], 2)
    pp[:, 73] = np.tile(inp["attn_k_norm"][l], 2)
    pp[:, 74] = np.tile(inp["diff_q_norm"][l], 2)
    pp[:, 75] = np.tile(inp["diff_k_norm"][l], 2)
    pp[:, 76] = inp["diff_out_norm"][l]
    pp[0:64, 77:81] = inp["diff_lambda"][l].T
    return pp


_CACHE = {}


def run(inputs, NTOK, SSEQ, core_x, core_is_sample, debug=False, stop=None):
    key = (NTOK, SSEQ, debug, stop)
    if key not in _CACHE:
        _CACHE[key] = build_program(NTOK, SSEQ, debug, stop)
    nc = _CACHE[key]
    NT = NTOK // T
    NCH = NTOK // SSEQ
    f = lambda a: np.ascontiguousarray(np.asarray(a, dtype=np.float32))
    shared = {nm: f(inputs[nm]) for nm in ("ffn1_w_in", "ffn1_w_out", "w_in", "pool_w", "w_branch", "w_out",
                                           "ffn2_w_in", "ffn2_w_out")}
    shared["pp"] = np.stack([_pack_params(inputs, l) for l in range(2)])
    shared["cst"] = _consts()
    tabs = {}
    for samp in (False, True):
        seg = SSEQ if samp else NTOK
        t = np.arange(NTOK) % seg
        rope = _rope_tables(t)
        mbt = np.zeros((128, NCH * NCH), np.float32)
        hft = np.ones((128, 2 * NT), np.float32)
        if samp:
            for a in range(NCH):
                for c in range(NCH):
                    if a != c:
                        mbt[:, a * NCH + c] = NEG
        for ti in range(NT):
            if (ti * T) % seg == 0:
                hft[:, 2 * ti] = 0.0
            if ((ti + 1) * T) % seg == 0:
                hft[:, 2 * ti + 1] = 0.0
        icnt = np.zeros((4, NTOK), np.float32)
        for gi, win in enumerate((2, 4, 8, 16)):
            lo = np.clip(t - win // 2, 0, seg)
            hi = np.clip(t - win // 2 + win, 0, seg)
            icnt[gi] = 1.0 / (hi - lo).astype(np.float32)
        tabs[samp] = dict(rope=rope, mb=mbt, hf=hft, icnt=icnt)
    in_maps = []
    for xc, samp in zip(core_x, core_is_sample):
        m = dict(shared)
        m.update(tabs[bool(samp)])
        m["x"] = f(xc)
        in_maps.append(m)
    res = run_bass_kernel_spmd(nc, in_maps, core_ids=list(range(len(in_maps))), trace=bool(os.environ.get('KTRACE')))
    return res


def kernel(**inputs):
    xp = np.asarray(inputs["x_prompt"], dtype=np.float32)
    xs = np.asarray(inputs["x_sample"], dtype=np.float32)
    Bp, Sp, _ = xp.shape
    Bs, Ss, _ = xs.shape
    per = Sp // Ss
    core_x = [xp[i] for i in range(Bp)] + [xs[i * per:(i + 1) * per].reshape(Sp, D) for i in range(Bs // per)]
    flags = [False] * Bp + [True] * (Bs // per)
    res = run(inputs, Sp, Ss, core_x, flags)
    ys = [np.asarray(r["y"], dtype=np.float32) for r in res.results]
    y_prompt = np.stack(ys[:Bp]).reshape(Bp, Sp, D)
    y_sample = np.concatenate(ys[Bp:], axis=0).reshape(Bs, Ss, D)
    return (y_prompt, y_sample)
```

```python
import math
import os
from contextlib import ExitStack

import numpy as np
import concourse.bass as bass
import concourse.mybir as mybir
from concourse.bass_utils import run_bass_kernel_spmd

F32 = mybir.dt.float32
BF16 = mybir.dt.bfloat16
AF = mybir.ActivationFunctionType
ALU = mybir.AluOpType

D = 1024
DFF = 2816
INW = 8448
T = 512
EPS = 1e-6
NSLOT = 6
SLOTW = 8 * 512
NPP = 88
NEG = -30000.0


class Dep:
    __slots__ = ("w", "r", "dsem", "accum")

    def __init__(self, accum=False):
        self.w = {}
        self.r = {}
        self.dsem = {}
        self.accum = accum


class TV:
    __slots__ = ("ap", "dep", "sbuf")

    def __init__(self, ap, dep=None, sbuf=True):
        self.ap = ap
        self.dep = dep if dep is not None else Dep()
        self.sbuf = sbuf

    def __getitem__(self, k):
        return TV(self.ap[k], self.dep, self.sbuf)

    def re(self, s, **kw):
        return TV(self.ap.rearrange(s, **kw), self.dep, self.sbuf)


class Bld:
    ENG = ("pe", "act", "dve", "pool", "sp")

    def __init__(self, nc, es, ndsem=90):
        self.nc = nc
        self.ops = {k: [] for k in self.ENG}
        self.cnt = {k: 0 for k in ("pe", "act", "dve", "pool")}
        self.seen = {k: {} for k in self.ENG}
        self.esem = {k: es.enter_context(nc.semaphore("s_" + k)) for k in self.cnt}
        self.free = [es.enter_context(nc.semaphore("d%d" % i)) for i in range(ndsem)]
        self.dsems = []

    def new_dsem(self):
        so = self.free.pop()
        self.dsems.append([so, 0])
        return len(self.dsems) - 1

    def _wait(self, eng, key, val):
        if not isinstance(key, str):
            val = self.dsems[key][1]
        if val <= 0 or self.seen[eng].get(key, 0) >= val:
            return
        self.seen[eng][key] = val
        so = self.esem[key] if isinstance(key, str) else self.dsems[key][0]
        self.ops[eng].append(lambda e, so=so, val=val: e.wait_ge(so, val))

    def issue(self, eng, fn, reads, writes, sig=True):
        sig = True
        idx = self.cnt[eng] + 1
        for tv in reads:
            for k, v in tv.dep.w.items():
                if k == eng:
                    if eng != "pe":
                        self._wait(eng, k, v)
                else:
                    self._wait(eng, k, v)
        for tv in writes:
            d = tv.dep
            for k, v in d.w.items():
                if k != eng:
                    self._wait(eng, k, v)
            for k, v in d.r.items():
                if k != eng:
                    self._wait(eng, k, v)
        for tv in reads:
            tv.dep.r[eng] = idx
        for tv in writes:
            tv.dep.w = {eng: idx}
            tv.dep.r = {}
        if sig:
            sem = self.esem[eng]
            self.ops[eng].append(lambda e, fn=fn, sem=sem: fn(e).then_inc(sem, 1))
            self.cnt[eng] = idx
        else:
            self.ops[eng].append(fn)

    def dma(self, q, out, in_, key=None):
        sb = out if out.sbuf else (in_ if in_.sbuf else None)
        if key is None:
            if q not in sb.dep.dsem:
                sb.dep.dsem[q] = self.new_dsem()
            key = sb.dep.dsem[q]
        for k, v in in_.dep.w.items():
            self._wait(q, k, v)
        if not out.dep.accum:
            for k, v in out.dep.w.items():
                if k != key:
                    self._wait(q, k, v)
        for k, v in out.dep.r.items():
            self._wait(q, k, v)
        self.dsems[key][1] += 16
        val = self.dsems[key][1]
        so = self.dsems[key][0]
        oa, ia = out.ap, in_.ap
        self.ops[q].append(lambda e: e.dma_start(out=oa, in_=ia).then_inc(so, 16))
        if not in_.dep.accum:
            in_.dep.r[key] = val
        if out.dep.accum:
            out.dep.w[key] = val
        else:
            out.dep.w = {key: val}
            out.dep.r = {}

    def barrier(self):
        for eng in self.ENG:
            for k in self.cnt:
                if k != eng:
                    self._wait(eng, k, self.cnt[k])
            for i, (so, v) in enumerate(self.dsems):
                self._wait(eng, i, v)

    def mm(self, out, lhsT, rhs, start, stop, sig=None):
        sig = stop if sig is None else sig
        o, l, r = out.ap, lhsT.ap, rhs.ap
        self.issue("pe", lambda e: e.matmul(o, l, r, start=start, stop=stop), [lhsT, rhs], [out], sig)

    def tr(self, out, in_, ident, sig=True):
        o, i, d = out.ap, in_.ap, ident.ap
        self.issue("pe", lambda e: e.transpose(o, i, d), [in_, ident], [out], sig)

    def act(self, out, in_, func, bias=None, scale=None):
        reads = [in_]
        kw = {}
        if bias is not None:
            if isinstance(bias, TV):
                reads.append(bias)
                kw["bias"] = bias.ap
            else:
                kw["bias"] = float(bias)
        if scale is not None:
            if isinstance(scale, TV):
                reads.append(scale)
                kw["scale"] = scale.ap
            else:
                kw["scale"] = float(scale)
        o, i = out.ap, in_.ap
        self.issue("act", lambda e: e.activation(out=o, in_=i, func=func, **kw), reads, [out])

    def tt(self, out, in0, in1, op, eng="dve"):
        o, a, b = out.ap, in0.ap, in1.ap
        self.issue(eng, lambda e: e.tensor_tensor(out=o, in0=a, in1=b, op=op), [in0, in1], [out])

    def ts(self, out, in0, s1, op0, s2=None, op1=None, eng="dve"):
        reads = [in0]
        a1 = s1
        if isinstance(s1, TV):
            reads.append(s1)
            a1 = s1.ap
        a2 = s2
        if isinstance(s2, TV):
            reads.append(s2)
            a2 = s2.ap
        o, a = out.ap, in0.ap
        if op1 is None:
            self.issue(eng, lambda e: e.tensor_scalar(o, a, a1, None, op0), reads, [out])
        else:
            self.issue(eng, lambda e: e.tensor_scalar(o, a, a1, a2, op0, op1), reads, [out])

    def stt(self, out, in0, scalar, in1, op0, op1):
        reads = [in0, in1]
        sc = scalar
        if isinstance(scalar, TV):
            reads.append(scalar)
            sc = scalar.ap
        o, a, b = out.ap, in0.ap, in1.ap
        self.issue("dve", lambda e: e.scalar_tensor_tensor(out=o, in0=a, scalar=sc, in1=b, op0=op0, op1=op1),
                   reads, [out])

    def recip(self, out, in_):
        o, i = out.ap, in_.ap
        self.issue("dve", lambda e: e.reciprocal(out=o, in_=i), [in_], [out])

    def copy(self, out, in_, eng="dve"):
        o, i = out.ap, in_.ap
        self.issue(eng, lambda e: e.tensor_copy(out=o, in_=i), [in_], [out])

    def memset(self, out, val, eng="dve"):
        o = out.ap
        self.issue(eng, lambda e: e.memset(o, val), [], [out])


def build_program(NTOK, SSEQ, debug=False, stop=None):
    NT = NTOK // T
    NB = NTOK // 128
    NCH = NTOK // SSEQ
    nc = bass.Bass("TRN2", target_bir_lowering=False)
    es = ExitStack()

    def din(name, shape, dt=F32):
        return TV(nc.dram_tensor(name, list(shape), dt, kind="ExternalInput").ap(), Dep(accum=True), sbuf=False)

    def dscr(name, shape, dt):
        kind = "ExternalOutput" if debug else "Internal"
        return TV(nc.dram_tensor(name, list(shape), dt, kind=kind).ap(), Dep(accum=True), sbuf=False)

    x_in = din("x", [NTOK, D])
    y_out = TV(nc.dram_tensor("y", [NTOK, D], F32, kind="ExternalOutput").ap(), Dep(accum=True), sbuf=False)
    W = {}
    for nm, shp in (("ffn1_w_in", [2, D, 2 * DFF]), ("ffn1_w_out", [2, DFF, D]), ("w_in", [2, D, INW]),
                    ("pool_w", [2, 4, 128, 128]), ("w_branch", [2, 4, 512, D]), ("w_out", [2, D, D]),
                    ("ffn2_w_in", [2, D, 2 * DFF]), ("ffn2_w_out", [2, DFF, D])):
        W[nm] = din(nm, shp)
    pp_in = din("pp", [2, 128, NPP])
    cst_in = din("cst", [5, 128, 128])
    rope_in = din("rope", [4, 64, NTOK])
    mb_in = din("mb", [128, NCH * NCH])
    hf_in = din("hf", [128, 2 * NT])
    icnt_in = din("icnt", [4, NTOK])

    WS = []
    for l in range(2):
        WS.append(dict(
            f1i=dscr("ws_f1i%d" % l, [128, 11, 8, 512], BF16), f1o=dscr("ws_f1o%d" % l, [128, 8, 22, 128], BF16),
            f2i=dscr("ws_f2i%d" % l, [128, 11, 8, 512], BF16), f2o=dscr("ws_f2o%d" % l, [128, 8, 22, 128], BF16),
            win=dscr("ws_in%d" % l, [128, 17, 8, 512], BF16), br=dscr("ws_br%d" % l, [128, 4, 4, 1024], BF16),
            out=dscr("ws_out%d" % l, [128, 2, 8, 512], BF16)))
    SC = []
    for l in range(2):
        SC.append(dict(
            X1=dscr("X1_%d" % l, [8, 128, NTOK], F32), XS=dscr("XS_%d" % l, [8, 128, NTOK], F32),
            CZ=dscr("CZ_%d" % l, [4, 128, NTOK], F32), PIN=dscr("PIN_%d" % l, [4, 128, NTOK], F32),
            QG=dscr("QG_%d" % l, [4, 128, NTOK], BF16), KG=dscr("KG_%d" % l, [128, NTOK], BF16),
            VG=dscr("VG_%d" % l, [128, 2, NB, 128], BF16),
            QD=dscr("QD_%d" % l, [4, 128, NTOK], BF16), KD=dscr("KD_%d" % l, [4, 128, NTOK], BF16),
            VD=dscr("VD_%d" % l, [128, 4, NB, 128], BF16),
            OG=dscr("OG_%d" % l, [4, 128, NTOK], BF16), OD=dscr("OD_%d" % l, [4, 128, NTOK], BF16)))

    AW = 206 * 256
    arena = es.enter_context(nc.sbuf_tensor("arena", [128, AW], F32))
    psall = es.enter_context(nc.psum_tensor("psall", [128, 4096], F32))
    psb = [TV(psall[:, i * 512:(i + 1) * 512], Dep()) for i in range(8)]
    pspair = [TV(psall[:, i * 1024:(i + 1) * 1024], Dep()) for i in range(2)]
    state = {"off": 0, "bank": 0}

    def alloc(nbytes):
        o = state["off"]
        state["off"] = o + (nbytes + 3) // 4
        assert state["off"] <= AW, ("arena overflow", state["off"] * 4)
        return o

    def f32t(n):
        o = alloc(4 * n)
        return arena[:, o:o + n]

    def bf16t(n):
        o = alloc(2 * n)
        return arena[:, o:o + (n + 1) // 2].bitcast(BF16)

    def group(tvs):
        g = {}
        for tv in tvs:
            tv.dep.dsem = g
        return tvs

    def chunks(ap3, n):
        return group([TV(ap3[:, c, :], Dep()) for c in range(n)])

    b = Bld(nc, es)

    def bank():
        k = state["bank"]
        state["bank"] = (k + 1) % 8
        return psb[k]

    ident = TV(f32t(128))
    cbf = bf16t(4 * 128).rearrange("p (a c) -> p a c", a=4)
    ones_t, bd_t, rtax_t, rtpr_t = [TV(cbf[:, i, :]) for i in range(4)]
    pp = [TV(f32t(NPP)) for _ in range(2)]
    poolw = [TV(bf16t(512).rearrange("p (g d) -> p g d", g=4)) for _ in range(2)]
    mb = TV(f32t(NCH * NCH))
    hf = TV(f32t(2 * NT))
    lamt = [TV(f32t(8)) for _ in range(2)]
    ognorm = [TV(f32t(1)) for _ in range(2)]
    group([ident, ones_t, bd_t, rtax_t, rtpr_t, mb, hf] + pp + poolw)
    ring = [TV(bf16t(SLOTW)) for _ in range(NSLOT)]
    wstate = {"i": 0}
    base_off = state["off"]

    def getw(src, kc, cb):
        s = ring[wstate["i"] % NSLOT]
        wstate["i"] += 1
        v = TV(s.ap[:, 0:kc * cb].rearrange("p (k c) -> p k c", k=kc), s.dep)
        b.dma("sp", v, src)
        if pending:
            pending.pop(0)()
        return v

    pending = []

    def pdma(out, in_):
        pending.append(lambda: b.dma("sp", out, in_))

    def flush_pending():
        while pending:
            pending.pop(0)()

    C_F1N, C_MXN, C_F2N, C_BG, C_CW, C_PS, C_AQ, C_AK, C_DQ, C_DK, C_DO, C_LV = 0, 8, 16, 24, 56, 68, 72, 73, 74, 75, 76, 77

    b.dma("sp", ident, cst_in[0])
    for i, tvv in enumerate((ones_t, bd_t, rtax_t, rtpr_t)):
        b.dma("pool", tvv, cst_in[1 + i])
    b.dma("sp", mb, mb_in)
    b.dma("sp", hf, hf_in)
    for l in range(2):
        b.dma("sp", pp[l], pp_in[l])
        b.dma("pool", poolw[l], W["pool_w"][l].re("g c d -> c g d"))

    conv_thunks = []

    def convert_layer(l, defer=False):
        ws = WS[l]

        def cv_dma(dst, src, key):
            if defer:
                conv_thunks.append(lambda: b.dma("pool", dst, src, key=key))
            else:
                b.dma("pool", dst, src, key=key)

        def cv_wait(key):
            if defer:
                conv_thunks.append(lambda: b._wait("pool", key, 1))
            else:
                b._wait("pool", key, 1)

        for nm_i, nm_o, wi, wo in (("f1i", "f1o", "ffn1_w_in", "ffn1_w_out"), ("f2i", "f2o", "ffn2_w_in", "ffn2_w_out")):
            if nm_i == "f2i":
                k = b.new_dsem(); pass
                segs = ((0, 1, 0, 512), (1, 4, 1024, 1536), (4, 5, 2560, 256), (5, 8, 2816, 1536), (8, 9, 512, 512),
                        (9, 17, 4352, 4096))
                for kc in range(8):
                    for b0, b1, c0, wd in segs:
                        src = W["w_in"][l, kc * 128:(kc + 1) * 128, c0:c0 + wd]
                        if b1 - b0 == 1:
                            dst = ws["win"][:, b0, kc, 0:wd]
                        else:
                            dst = ws["win"][:, b0:b1, kc, :]
                            src = src.re("p (b c) -> p b c", b=b1 - b0)
                        cv_dma(dst, src, k)
                    if kc % 2 == 1:
                        cv_wait(k)
                k = b.new_dsem(); pass
                for n in range(4):
                    cv_dma(ws["br"][:, n, :, :], W["w_branch"][l, n].re("(k p) c -> p k c", p=128), k)
                k = b.new_dsem(); pass
                for kc in range(8):
                    cv_dma(ws["out"][:, :, kc, :],
                           W["w_out"][l, kc * 128:(kc + 1) * 128, :].re("p (b c) -> p b c", b=2), k)
            k = b.new_dsem(); pass
            for kc in range(8):
                for half in range(2):
                    src = W[wi][l, kc * 128:(kc + 1) * 128, half * DFF:(half + 1) * DFF].re("p (b c) -> p b c", b=11)
                    cv_dma(ws[nm_i][:, :, kc, half * 256:(half + 1) * 256], src, k)
                if kc % 4 == 3:
                    cv_wait(k)
            k = b.new_dsem(); pass
            for kc in range(22):
                src = W[wo][l, kc * 128:(kc + 1) * 128, :].re("p (b c) -> p b c", b=8)
                cv_dma(ws[nm_o][:, :, kc, :], src, k)
                if kc % 8 == 7:
                    cv_wait(k)

    convert_layer(0)

    def lam_setup(l):
        lambda_init = 0.8 - 0.6 * math.exp(-0.3 * l)
        prod = TV(lamt[l].ap[:, 2:4], lamt[l].dep)
        b.tt(prod, pp[l][:, C_LV:C_LV + 4:2], pp[l][:, C_LV + 1:C_LV + 4:2], ALU.mult)
        pb = bank()
        b.mm(TV(pb.ap[:, 0:2], pb.dep), ident_ones, prod, True, True)
        ex = TV(lamt[l].ap[:, 4:6], lamt[l].dep)
        b.act(ex, TV(pb.ap[:, 0:2], pb.dep), AF.Exp)
        d = TV(lamt[l].ap[:, 6:7], lamt[l].dep)
        b.tt(d, ex[:, 1:2], ex[:, 0:1], ALU.subtract)
        b.ts(TV(lamt[l].ap[:, 0:1], lamt[l].dep), d, -lambda_init, ALU.add)
        b.ts(ognorm[l], pp[l][:, C_DO:C_DO + 1], 1.0 - lambda_init, ALU.mult)

    ones32 = TV(f32t(128))
    ident_ones = ones32
    b.memset(ones32, 1.0)
    epst = TV(f32t(1))
    b.memset(epst, EPS)
    base_off = state["off"]
    for l in range(2):
        lam_setup(l)

    def rmsnorm(xT, gcol, l, uT, sq, tmpa, tmpb):
        ssb = bank()
        for c in range(8):
            b.act(hT[2 * c], xT[c], AF.Square)
            b.mm(ssb, ones_t, hT[2 * c], c == 0, c == 7)
        b.act(tmpa, ssb, AF.Sqrt, bias=EPS, scale=1.0 / D)
        b.recip(tmpb, tmpa)
        for c in range(8):
            b.stt(uT[c], xT[c], pp[l][:, gcol + c:gcol + c + 1], tmpb, ALU.mult, ALU.mult)

    def ffn(xT, uT, hT, wsi, wso, silu_t):
        for j in range(11):
            wb = getw(wsi[:, j, :, :], 8, 512)
            for hh in range(2):
                hc = 2 * j + hh
                pa = bank()
                for kc in range(8):
                    b.mm(pa, wb[:, kc, hh * 128:(hh + 1) * 128], uT[kc], kc == 0, kc == 7)
                pbk = bank()
                for kc in range(8):
                    b.mm(pbk, wb[:, kc, 256 + hh * 128:256 + (hh + 1) * 128], uT[kc], kc == 0, kc == 7)
                st = silu_t[hc % 2]
                b.act(st, pa, AF.Silu)
                b.tt(hT[hc], st, pbk, ALU.mult)
        for oc in range(8):
            wb = getw(wso[:, oc, :, :], 22, 128)
            py = bank()
            for hc in range(22):
                b.mm(py, wb[:, hc, :], hT[hc], hc == 0, hc == 21)
            b.stt(xT[oc], py, 0.5, xT[oc], ALU.mult, ALU.add)

    state["off"] = base_off
    xTb = [chunks(f32t(8 * T).rearrange("p (c t) -> p c t", c=8), 8) for _ in range(2)]
    xTall = None
    uT = chunks(bf16t(8 * T).rearrange("p (c t) -> p c t", c=8), 8)
    _ho = alloc(22 * T * 2)
    _hap = arena[:, _ho:_ho + 11 * T].bitcast(BF16).rearrange("p (c t) -> p c t", c=22)
    _map = arena[:, _ho:_ho + 8 * T].rearrange("p (c t) -> p c t", c=8)
    _hd = [Dep() for _ in range(8)]
    hT = [TV(_hap[:, c, :], _hd[c // 2] if c < 16 else Dep()) for c in range(22)]
    merged = [TV(_map[:, c, :], _hd[c]) for c in range(8)]
    sq = [TV(bf16t(T)) for _ in range(2)]
    tmp = [TV(f32t(T)) for _ in range(8)]
    shared_off = state["off"]

    xtok = TV(f32t(4 * D).rearrange("p (a d) -> p a d", a=4))
    ropet2 = [group([TV(f32t(T)) for _ in range(4)]) for _ in range(2)]
    qk_st_ap = bf16t(13 * T).rearrange("p (c t) -> p c t", c=13)
    qk_st = chunks(qk_st_ap, 13)
    vg_st = TV(bf16t(2 * 4 * 128).rearrange("p (k a d) -> p k a d", k=2, a=4))
    vd_st = TV(bf16t(4 * 4 * 128).rearrange("p (h a d) -> p h a d", h=4, a=4))
    cz_st_ap = f32t(4 * T).rearrange("p (c t) -> p c t", c=4)
    cz_st = chunks(cz_st_ap, 4)
    pin_st_ap = f32t(4 * T).rearrange("p (c t) -> p c t", c=4)
    pin_st = chunks(pin_st_ap, 4)
    a_end = state["off"]

    state["off"] = shared_off
    xtok_c = [TV(f32t(D)) for _ in range(2)]
    merged_bf = chunks(bf16t(8 * T).rearrange("p (c t) -> p c t", c=8), 8)
    czh_ap = f32t(4 * (T + 2)).rearrange("p (c t) -> p c t", c=4)
    czh = TV(czh_ap)
    pinh_ap = f32t(4 * (T + 16)).rearrange("p (c t) -> p c t", c=4)
    pinh = TV(pinh_ap)
    icnt_t = TV(f32t(4 * T).rearrange("p (c t) -> p c t", c=4))
    o_ld_ap = bf16t(8 * T).rearrange("p (c t) -> p c t", c=8)
    o_ld = TV(o_ld_ap)
    yA = chunks(bf16t(4 * T).rearrange("p (c t) -> p c t", c=4), 4)
    yB = chunks(bf16t(4 * T).rearrange("p (c t) -> p c t", c=4), 4)
    dw = [TV(f32t(T + 16)) for _ in range(3)]
    m_bf4 = [TV(bf16t(T)) for _ in range(4)]
    c_end = state["off"]

    state["off"] = base_off
    kT = [TV(bf16t(NTOK)) for _ in range(2)]
    vT = [TV(bf16t(NB * 128).rearrange("p (n d) -> p n d", n=NB)) for _ in range(2)]
    qT = [TV(bf16t(NTOK)) for _ in range(2)]
    pTp = [TV(bf16t(2 * T)) for _ in range(4)]
    sqd = [TV(bf16t(T)) for _ in range(2)]
    btmp = [TV(f32t(T)) for _ in range(6)]
    ost = [TV(bf16t(T)) for _ in range(2)]
    b_end = state["off"]
    assert max(a_end, c_end, b_end) <= AW

    qctr = {"i": 0}

    def qknorm_jobs(ps, gcol, l, ax, out, ropet):
        cosT, sinT = (ropet[0], ropet[1]) if ax else (ropet[2], ropet[3])
        rt = rtax_t if ax else rtpr_t
        st = {}

        def s0():
            qctr["i"] += 1
            st["sq"] = hT[8 + (2 * qctr["i"]) % 14]
            st["qb"] = hT[8 + (2 * qctr["i"] + 1) % 14]
            b.act(st["sq"], ps, AF.Square)

        def s1():
            k = st["k"]
            st["ss"] = bank()
            b.mm(st["ss"], bd_t, st["sq"], True, True)
            b.act(tmp[k], st["ss"], AF.Sqrt, bias=EPS, scale=1.0 / 64)
            b.recip(tmp[k + 1], tmp[k])
            b.stt(tmp[k], ps, pp[l][:, gcol:gcol + 1], tmp[k + 1], ALU.mult, ALU.mult)

        def s2():
            b.act(st["qb"], tmp[st["k"]], AF.Copy)

        def s3():
            k = st["k"]
            st["rot"] = bank()
            b.mm(st["rot"], rt, st["qb"], True, True)
            b.tt(tmp[k], tmp[k], cosT, ALU.mult, eng="pool")
            b.tt(tmp[k + 1], st["rot"], sinT, ALU.mult)
            b.tt(out, tmp[k], tmp[k + 1], ALU.add, eng="pool")

        return st, [s0, s1, s2, s3]

    SUB = int(os.environ.get("KSUB", "99"))

    def phase_A(l):
        ws, sc = WS[l], SC[l]
        b.memset(TV(vg_st.ap[:, :, :, 64:128], vg_st.dep), 1.0)
        def loads_A(t, dma=None):
            dma = dma or (lambda o, i: b.dma("sp", o, i))
            tok = slice(t * T, (t + 1) * T)
            if l == 0:
                dma(xtok, x_in[tok, :].re("(a p) d -> p a d", p=128))
            else:
                for c in range(8):
                    dma(xTb[t % 2][c], SC[l - 1]["XS"][c, :, tok])
            for i in range(4):
                r = ropet2[t % 2][i]
                dma(r[0:64, :], rope_in[i, :, tok])
                dma(r[64:128, :], rope_in[i, :, tok])

        loads_A(0)
        for t in range(NT):
            flush_pending()
            xT = xTb[t % 2]
            ropet = ropet2[t % 2]
            tok = slice(t * T, (t + 1) * T)
            if l == 0:
                for c in range(8):
                    pb = bank()
                    for a in range(4):
                        b.tr(TV(pb.ap[:, a * 128:(a + 1) * 128], pb.dep), xtok[:, a, c * 128:(c + 1) * 128], ident,
                             sig=(a == 3))
                    if c % 2 == 0:
                        b.act(xT[c], pb, AF.Copy)
                    else:
                        b.copy(xT[c], pb)
            if SUB <= 0:
                return
            rmsnorm(xT, C_F1N, l, uT, sq, tmp[0], tmp[1])
            if SUB <= 1:
                return
            ffn(xT, uT, hT, ws["f1i"], ws["f1o"], tmp[2:4])
            if t + 1 < NT:
                loads_A(t + 1, pdma)
            for c in range(8):
                b.dma("pool", sc["X1"][c, :, tok], xT[c])
            rmsnorm(xT, C_MXN, l, uT, sq, tmp[0], tmp[1])
            if SUB <= 3:
                return

            def proj(wb, col0):
                pb = bank()
                for kc in range(8):
                    b.mm(pb, wb[:, kc, col0:col0 + 128], uT[kc], kc == 0, kc == 7)
                return pb

            wb = getw(ws["win"][:, 0, :, :], 8, 512)
            for c in range(4):
                pb = proj(wb, c * 128)
                b.act(cz_st[c], pb, AF.Copy)
            wb = getw(ws["win"][:, 1, :, :], 8, 512)
            for c in range(4):
                pb = proj(wb, c * 128)
                b.tt(cz_st[c], cz_st[c], pb, ALU.mult)
                b.dma("pool", sc["CZ"][c, :, tok], cz_st[c])
            wb = getw(ws["win"][:, 2, :, :], 8, 512)
            for c in range(4):
                pb = proj(wb, c * 128)
                if c % 2 == 0:
                    b.act(pin_st[c], pb, AF.Copy)
                else:
                    b.copy(pin_st[c], pb)
                b.dma("pool", sc["PIN"][c, :, tok], pin_st[c])
            if SUB <= 4:
                return
            pend = []

            def pump(newjob=None):
                if newjob is not None:
                    pend.append([newjob[0], list(newjob[1])])
                for jb in list(pend)[::-1]:
                    pass
                for jb in list(pend):
                    jb[1].pop(0)()
                    if not jb[1]:
                        pend.remove(jb)

            kslot = {"i": 0}

            def qk_chunk(wb, col0, gcol, ax, out):
                pb = proj(wb, col0)
                stt_, stages = qknorm_jobs(pb, gcol, l, ax, out, ropet)
                stt_["k"] = 2 * (kslot["i"] % 4)
                kslot["i"] += 1
                pump((stt_, stages))

            wb = getw(ws["win"][:, 3, :, :], 8, 512)
            for c in range(4):
                qk_chunk(wb, c * 128, C_AQ, True, qk_st[c])
            wb = getw(ws["win"][:, 4, :, :], 8, 512)
            qk_chunk(wb, 0, C_AK, True, qk_st[4])
            pbv = bank()
            for a in range(4):
                for kc in range(8):
                    b.mm(TV(pbv.ap[:, a * 128:(a + 1) * 128], pbv.dep), uT[kc][:, a * 128:(a + 1) * 128],
                         wb[:, kc, 128:256], kc == 0, kc == 7, sig=(kc == 7 and a == 3))
            b.act(TV(vg_st.ap[:, :, :, 0:64], vg_st.dep),
                  TV(pbv.ap.rearrange("p (a k d) -> p k a d", a=4, k=2), pbv.dep), AF.Copy)
            wb = getw(ws["win"][:, 5, :, :], 8, 512)
            for c in range(4):
                qk_chunk(wb, c * 128, C_DQ, False, qk_st[5 + c])
            wb = getw(ws["win"][:, 6, :, :], 8, 512)
            for c in range(4):
                qk_chunk(wb, c * 128, C_DK, False, qk_st[9 + c])
            wb = getw(ws["win"][:, 7, :, :], 8, 512)
            for a in range(4):
                pb = bank()
                for kc in range(8):
                    b.mm(pb, uT[kc][:, a * 128:(a + 1) * 128], wb[:, kc, :], kc == 0, kc == 7)
                dst = TV(vd_st.ap[:, :, a, :], vd_st.dep)
                src = TV(pb.ap.rearrange("p (h d) -> p h d", h=4), pb.dep)
                if a % 2 == 0:
                    b.act(dst, src, AF.Copy)
                else:
                    b.copy(dst, src)
                pump()
            while pend:
                pump()
            if SUB <= 5:
                return
            for c in range(4):
                b.dma("pool", sc["QG"][c, :, tok], qk_st[c])
                b.dma("pool", sc["QD"][c, :, tok], qk_st[5 + c])
                b.dma("pool", sc["KD"][c, :, tok], qk_st[9 + c])
            b.dma("pool", sc["KG"][:, tok], qk_st[4])
            b.dma("pool", sc["VG"][:, :, 4 * t:4 * t + 4, :], vg_st)
            b.dma("pool", sc["VD"][:, :, 4 * t:4 * t + 4, :], vd_st)

    def phase_B(l):
        sc = SC[l]
        scale = 1.0 / 8.0
        SKEW = 3
        s_banks = [psb[0], psb[1], psb[2], psb[3]]
        it = {"s": 0, "p": 0, "g": 0}
        deferred = []

        def run_pipe(items, flush=False):
            n = len(items)
            per = max(1, n // 100)
            for i in range(n + SKEW + 3):
                if conv_thunks and i % per == 0:
                    conv_thunks.pop(0)()
                if i < n:
                    items[i][0]()
                for dl in [d for d in deferred if d[0] <= i]:
                    deferred.remove(dl)
                    dl[1]()
                if SKEW <= i < n + SKEW:
                    items[i - SKEW][1](i)
            assert not deferred
            while flush and conv_thunks:
                conv_thunks.pop(0)()

        items = []
        for j in range(2):
            for g in range(0, 4, 2):
                ch = (4 * j + g) // 2
                for qi in range(NT):
                    for kb in range(NB):
                        def s1(j=j, g=g, ch=ch, qi=qi, kb=kb, st={}):
                            kt, vt = kT[j % 2], vT[j % 2]
                            if g == 0 and qi == 0 and kb == 0:
                                b.dma("sp", kt[0:64, :], sc["KG"][j * 64:(j + 1) * 64, :])
                                b.dma("sp", kt[64:128, :], sc["KG"][j * 64:(j + 1) * 64, :])
                                b.dma("sp", vt, sc["VG"][:, j, :, :])
                            qt_ = qT[ch % 2]
                            if qi == 0 and kb == 0:
                                b.dma("sp", qt_, sc["QG"][ch, :, :])
                            qc = (qi * T) // SSEQ
                            kc_ = (kb * 128) // SSEQ
                            bias = mb[:, qc * NCH + kc_:qc * NCH + kc_ + 1]
                            sp_ = pspair[it["s"] % 2]
                            it["s"] += 1
                            for hh in range(2):
                                pr = slice(hh * 64, (hh + 1) * 64)
                                b.mm(sp_[:, hh * T:(hh + 1) * T], kt[pr, kb * 128:(kb + 1) * 128],
                                     qt_[pr, qi * T:(qi + 1) * T], True, True)
                            pp_ = pTp[it["p"] % 4]
                            it["p"] += 1
                            b.act(pp_, sp_, AF.Exp, bias=bias, scale=scale)
                            st["p"] = [pp_[:, 0:T], pp_[:, T:2 * T]]

                        def s2(i_, j=j, ch=ch, qi=qi, kb=kb, st=s1.__defaults__[-1]):
                            vt = vT[j % 2]
                            if kb == 0:
                                it["g"] += 1
                            gi_ = it["g"]
                            obs = [psb[4 + 2 * (gi_ % 2)], psb[5 + 2 * (gi_ % 2)]]
                            for hh in range(2):
                                b.mm(obs[hh], vt[:, kb, :], st["p"][hh], kb == 0, kb == NB - 1)
                            if kb == NB - 1:
                                for hh in range(2):
                                    pr = slice(hh * 64, (hh + 1) * 64)
                                    rc = btmp[hh]
                                    b.recip(rc[0:64, :], obs[hh][64:128, :])
                                    os_ = ost[hh]
                                    b.tt(os_[0:64, :], obs[hh][0:64, :], rc[0:64, :], ALU.mult)
                                    b.dma("pool", sc["OG"][ch, pr, qi * T:(qi + 1) * T], os_[0:64, :])
                        items.append((s1, s2))
        run_pipe(items)

        acc = [psb[4], psb[5], psb[6], psb[7]]
        items = []
        for h in range(4):
            for qi in range(NT):
                for kb in range(NB):
                    def s1(h=h, qi=qi, kb=kb, st={}):
                        kt, vt, qt_ = kT[h % 2], vT[h % 2], qT[h % 2]
                        if qi == 0 and kb == 0:
                            b.dma("sp", kt, sc["KD"][h, :, :])
                            b.dma("sp", vt, sc["VD"][:, h, :, :])
                            b.dma("sp", qt_, sc["QD"][h, :, :])
                        qc = (qi * T) // SSEQ
                        kc_ = (kb * 128) // SSEQ
                        bias = mb[:, qc * NCH + kc_:qc * NCH + kc_ + 1]
                        sp_ = pspair[it["s"] % 2]
                        it["s"] += 1
                        for cmp_ in range(2):
                            pr = slice(cmp_ * 64, (cmp_ + 1) * 64)
                            b.mm(sp_[:, cmp_ * T:(cmp_ + 1) * T], kt[pr, kb * 128:(kb + 1) * 128],
                                 qt_[pr, qi * T:(qi + 1) * T], True, True)
                        pp_ = pTp[it["p"] % 4]
                        it["p"] += 1
                        b.act(pp_, sp_, AF.Exp, bias=bias, scale=scale)
                        st["p"] = [pp_[:, 0:T], pp_[:, T:2 * T]]

                    def s2(i_, h=h, qi=qi, kb=kb, st=s1.__defaults__[-1]):
                        vt = vT[h % 2]
                        st_, sp_ = kb == 0, kb == NB - 1
                        for cmp_ in range(2):
                            b.mm(acc[cmp_], vt[:, kb, :], st["p"][cmp_], st_, sp_)
                        for cmp_ in range(2):
                            b.mm(acc[2 + cmp_], ones_t, st["p"][cmp_], st_, sp_)
                        if kb == NB - 1:
                            b.recip(btmp[0], acc[2])
                            b.recip(btmp[1], acc[3])
                            b.tt(btmp[0], acc[0], btmp[0], ALU.mult)
                            b.tt(btmp[1], acc[1], btmp[1], ALU.mult)
                            b.stt(btmp[2], btmp[1], lamt[l][:, 0:1], btmp[0], ALU.mult, ALU.add)
                            sqv = sqd[qi % 2]
                            b.act(sqv, btmp[2], AF.Square)

                            def fin2(h=h, qi=qi, sqv=sqv):
                                ssb = pspair[it["s"] % 2][:, 0:T]
                                it["s"] += 1
                                b.mm(ssb, ones_t, sqv, True, True)
                                b.act(btmp[3], ssb, AF.Ln, bias=epst[:, 0:1], scale=1.0 / 128)
                                b.act(btmp[4], btmp[3], AF.Exp, scale=-0.5)
                                os_ = ost[qi % 2]
                                b.stt(os_, btmp[2], ognorm[l], btmp[4], ALU.mult, ALU.mult)
                                b.dma("pool", sc["OD"][h, :, qi * T:(qi + 1) * T], os_)
                            deferred.append((i_ + 3, fin2))
                    items.append((s1, s2))
        run_pipe(items, flush=True)

    def phase_C(l, last):
        ws, sc = WS[l], SC[l]

        def loads_C(t, dma=None):
            dma = dma or (lambda o, i: b.dma("sp", o, i))
            xT = xTb[t % 2]
            tok = slice(t * T, (t + 1) * T)
            for c in range(8):
                dma(xT[c], sc["X1"][c, :, tok])
            lo, hi = t * T - 1, (t + 1) * T + 1
            if t == 0:
                b.memset(TV(czh.ap[:, :, 0:1], czh.dep), 0.0)
                dma(TV(czh.ap[:, :, 1:T + 2], czh.dep), sc["CZ"][:, :, 0:T + 1].re("c p t -> p c t"))
            elif t == NT - 1:
                b.memset(TV(czh.ap[:, :, T + 1:T + 2], czh.dep), 0.0)
                dma(TV(czh.ap[:, :, 0:T + 1], czh.dep), sc["CZ"][:, :, lo:lo + T + 1].re("c p t -> p c t"))
            else:
                dma(czh, sc["CZ"][:, :, lo:hi].re("c p t -> p c t"))
            lo, hi = t * T - 8, (t + 1) * T + 8
            if t == 0:
                b.memset(TV(pinh.ap[:, :, 0:8], pinh.dep), 0.0)
                dma(TV(pinh.ap[:, :, 8:T + 16], pinh.dep), sc["PIN"][:, :, 0:T + 8].re("c p t -> p c t"))
            elif t == NT - 1:
                b.memset(TV(pinh.ap[:, :, T + 8:T + 16], pinh.dep), 0.0)
                dma(TV(pinh.ap[:, :, 0:T + 8], pinh.dep), sc["PIN"][:, :, lo:lo + T + 8].re("c p t -> p c t"))
            else:
                dma(pinh, sc["PIN"][:, :, lo:hi].re("c p t -> p c t"))
            for gi in range(4):
                dma(TV(icnt_t.ap[:, gi:gi + 1, :], icnt_t.dep),
                      TV(icnt_in.ap[gi:gi + 1, tok].partition_broadcast(128), icnt_in.dep, False))
            dma(TV(o_ld.ap[:, 0:4, :], o_ld.dep), sc["OG"][:, :, tok].re("c p t -> p c t"))
            dma(TV(o_ld.ap[:, 4:8, :], o_ld.dep), sc["OD"][:, :, tok].re("c p t -> p c t"))

        loads_C(0)
        for t in range(NT):
            flush_pending()
            xT = xTb[t % 2]
            tok = slice(t * T, (t + 1) * T)
            b.ts(TV(czh.ap[:, :, 0:1], czh.dep), TV(czh.ap[:, :, 0:1], czh.dep), hf[:, 2 * t:2 * t + 1], ALU.mult)
            b.ts(TV(czh.ap[:, :, T + 1:T + 2], czh.dep), TV(czh.ap[:, :, T + 1:T + 2], czh.dep),
                 hf[:, 2 * t + 1:2 * t + 2], ALU.mult)
            b.ts(TV(pinh.ap[:, :, 0:8], pinh.dep), TV(pinh.ap[:, :, 0:8], pinh.dep), hf[:, 2 * t:2 * t + 1], ALU.mult)
            b.ts(TV(pinh.ap[:, :, T + 8:T + 16], pinh.dep), TV(pinh.ap[:, :, T + 8:T + 16], pinh.dep),
                 hf[:, 2 * t + 1:2 * t + 2], ALU.mult)
            rmsnorm(xT, C_MXN, l, uT, sq, tmp[0], tmp[1])

            def proj(wb, col0):
                pb = bank()
                for kc in range(8):
                    b.mm(pb, wb[:, kc, col0:col0 + 128], uT[kc], kc == 0, kc == 7)
                return pb

            for gi in range(4):
                pc = TV(pinh.ap[:, gi, :], pinh.dep)
                win = (2, 4, 8, 16)[gi]
                cur = pc
                n = T + 16
                step = 1
                k = 0
                while step < win:
                    nn = n - step
                    dst = dw[k % 3]
                    b.tt(dst[:, 0:nn], cur[:, 0:nn], cur[:, step:step + nn], ALU.add, eng="pool")
                    cur = dst
                    n = nn
                    step *= 2
                    k += 1
                off = 8 - win // 2
                mt = tmp[4 + (gi % 2)]
                b.tt(mt, cur[:, off:off + T], TV(icnt_t.ap[:, gi, :], icnt_t.dep), ALU.mult)
                b.tt(m_bf4[gi], mt, pc[:, 8:8 + T], ALU.subtract)
            wb = getw(ws["win"][:, 8, :, :], 8, 512)
            for c in range(4):
                pb = proj(wb, c * 128)
                cw = lambda k: pp[l][:, C_CW + k * 4 + c:C_CW + k * 4 + c + 1]
                tm = tmp[2 + (c % 2)]
                zc = TV(czh.ap[:, c, :], czh.dep)
                b.ts(tm, zc[:, 1:T + 1], cw(1), ALU.mult)
                b.stt(tm, zc[:, 0:T], cw(0), tm, ALU.mult, ALU.add)
                b.stt(tm, zc[:, 2:T + 2], cw(2), tm, ALU.mult, ALU.add)
                b.tt(yA[c], tm, pb, ALU.mult)
            for gi in range(4):
                pb = bank()
                b.mm(pb, poolw[l][:, gi, :], m_bf4[gi], True, True)
                b.act(yB[gi], pb, AF.Copy, scale=pp[l][:, C_PS + gi:C_PS + gi + 1])
            for n in range(4):
                if n == 0:
                    yn = yA
                elif n == 1:
                    yn = yB
                else:
                    yn = [TV(o_ld.ap[:, (n - 2) * 4 + kc, :], o_ld.dep) for kc in range(4)]
                wbr = getw(ws["br"][:, n, :, :], 4, 1024)
                for half in range(2):
                    wg = getw(ws["win"][:, 9 + 2 * n + half, :, :], 8, 512)
                    for o4 in range(4):
                        oc = half * 4 + o4
                        pg = proj(wg, o4 * 128)
                        gt = tmp[2 + (oc % 2)]
                        b.act(gt, pg, AF.Sigmoid, bias=pp[l][:, C_BG + n * 8 + oc:C_BG + n * 8 + oc + 1])
                        pbr = bank()
                        for kc in range(4):
                            b.mm(pbr, wbr[:, kc, oc * 128:(oc + 1) * 128], yn[kc], kc == 0, kc == 3)
                        if n == 0:
                            b.tt(merged[oc], gt, pbr, ALU.mult)
                        else:
                            b.tt(gt, gt, pbr, ALU.mult)
                            b.tt(merged_bf[oc] if n == 3 else merged[oc], merged[oc], gt, ALU.add)
            for half in range(2):
                wo = getw(ws["out"][:, half, :, :], 8, 512)
                for o4 in range(4):
                    oc = half * 4 + o4
                    pb = bank()
                    for kc in range(8):
                        b.mm(pb, wo[:, kc, o4 * 128:(o4 + 1) * 128], merged_bf[kc], kc == 0, kc == 7)
                    b.tt(xT[oc], xT[oc], pb, ALU.add)
            if t + 1 < NT:
                loads_C(t + 1, pdma)
            rmsnorm(xT, C_F2N, l, uT, sq, tmp[0], tmp[1])
            ffn(xT, uT, hT, ws["f2i"], ws["f2o"], tmp[2:4])
            if last:
                for a in range(4):
                    for half in range(2):
                        pb = bank()
                        for c4 in range(4):
                            c = half * 4 + c4
                            b.tr(TV(pb.ap[:, c4 * 128:(c4 + 1) * 128], pb.dep), xT[c][:, a * 128:(a + 1) * 128], ident,
                                 sig=(c4 == 3))
                        dst = xtok_c[a % 2][:, half * 512:(half + 1) * 512]
                        if half == 0:
                            b.act(dst, pb, AF.Copy)
                        else:
                            b.copy(dst, pb)
                    b.dma("pool", y_out[t * T + a * 128:t * T + (a + 1) * 128, :], xtok_c[a % 2])
            else:
                for c in range(8):
                    b.dma("pool", sc["XS"][c, :, tok], xT[c])

    plan = [("A0", lambda: phase_A(0)), ("cv1", lambda: convert_layer(1, defer=True)), ("B0", lambda: phase_B(0)),
            ("C0", lambda: phase_C(0, False)), ("A1", lambda: phase_A(1)), ("B1", lambda: phase_B(1)),
            ("C1", lambda: phase_C(1, True))]
    b.barrier()
    for nm, fn in plan:
        if stop == "pro":
            break
        for eng in b.ENG:
            b.ops[eng].append(("scope", nm))
        fn()
        if nm != "cv1":
            b.barrier()
        if nm == stop:
            break

    def replay(e, lst):
        cur = None
        for f in lst:
            if isinstance(f, tuple):
                if cur is not None:
                    cur.__exit__(None, None, None)
                cur = nc.named_scope(f[1])
                cur.__enter__()
            else:
                f(e)
        if cur is not None:
            cur.__exit__(None, None, None)

    with nc.Block() as block:
        @block.sync
        def _(e):
            replay(e, b.ops["sp"])

        @block.tensor
        def _(e):
            replay(e, b.ops["pe"])

        @block.scalar
        def _(e):
            replay(e, b.ops["act"])

        @block.vector
        def _(e):
            replay(e, b.ops["dve"])

        @block.gpsimd
        def _(e):
            replay(e, b.ops["pool"])
    es.close()
    return nc


def _rope_tables(pos_in_seq):
    n_tok = pos_in_seq.shape[0]
    t = pos_in_seq
    row = (t // 64).astype(np.float32)
    col = (t % 64).astype(np.float32)
    pos = t.astype(np.float32)
    axc = np.ones((64, n_tok), np.float32)
    axs = np.zeros((64, n_tok), np.float32)
    fr = np.exp(np.float32(-math.log(10000.0)) * np.arange(16, dtype=np.float32) * np.float32(2.0 / 32)).astype(np.float32)
    for d in range(64):
        p = row if d < 32 else col
        i = d % 16
        ang = (p * fr[i]).astype(np.float32)
        axc[d] = np.cos(ang)
        sgn = -1.0 if (d % 32) < 16 else 1.0
        axs[d] = sgn * np.sin(ang)
    prc = np.ones((64, n_tok), np.float32)
    prs = np.zeros((64, n_tok), np.float32)
    fr2 = np.exp(np.float32(-math.log(500000.0)) * np.arange(8, dtype=np.float32) * np.float32(2.0 / 16)).astype(np.float32)
    for d in range(16):
        i = d % 8
        ang = (pos * fr2[i]).astype(np.float32)
        prc[d] = np.cos(ang)
        sgn = -1.0 if d < 8 else 1.0
        prs[d] = sgn * np.sin(ang)
    return np.stack([axc, axs, prc, prs]).astype(np.float32)


def _consts():
    ident = np.eye(128, dtype=np.float32)
    ones = np.ones((128, 128), np.float32)
    bd = np.zeros((128, 128), np.float32)
    bd[0:64, 0:64] = 1.0
    bd[64:128, 64:128] = 1.0
    rtax = np.zeros((128, 128), np.float32)
    rtpr = np.zeros((128, 128), np.float32)
    for m in range(128):
        hb, d = (m // 64) * 64, m % 64
        blk, i = d // 32, d % 32
        partner = hb + blk * 32 + (i + 16) % 32
        rtax[partner, m] = 1.0
        if d < 16:
            partner = hb + (d + 8) % 16
            rtpr[partner, m] = 1.0
    return np.stack([ident, ones, bd, rtax, rtpr]).astype(np.float32)


def _pack_params(inp, l):
    pp = np.zeros((128, NPP), np.float32)
    pp[:, 0:8] = inp["ffn1_norm"][l].reshape(8, 128).T
    pp[:, 8:16] = inp["mix_norm"][l].reshape(8, 128).T
    pp[:, 16:24] = inp["ffn2_norm"][l].reshape(8, 128).T
    pp[:, 24:56] = inp["b_gate"][l].reshape(32, 128).T
    pp[:, 56:68] = inp["conv_w"][l].reshape(3, 4, 128).transpose(2, 0, 1).reshape(128, 12)
    pp[:, 68:72] = inp["pool_scale"][l].reshape(4, 128).T
    pp[:, 72] = np.tile(inp["attn_q_norm"][l], 2)
    pp[:, 73] = np.tile(inp["attn_k_norm"][l], 2)
    pp[:, 74] = np.tile(inp["diff_q_norm"][l], 2)
    pp[:, 75] = np.tile(inp["diff_k_norm"][l], 2)
    pp[:, 76] = inp["diff_out_norm"][l]
    pp[0:64, 77:81] = inp["diff_lambda"][l].T
    return pp


_CACHE = {}


def run(inputs, NTOK, SSEQ, core_x, core_is_sample, debug=False, stop=None):
    key = (NTOK, SSEQ, debug, stop)
    if key not in _CACHE:
        _CACHE[key] = build_program(NTOK, SSEQ, debug, stop)
    nc = _CACHE[key]
    NT = NTOK // T
    NCH = NTOK // SSEQ
    f = lambda a: np.ascontiguousarray(np.asarray(a, dtype=np.float32))
    shared = {nm: f(inputs[nm]) for nm in ("ffn1_w_in", "ffn1_w_out", "w_in", "pool_w", "w_branch", "w_out",
                                           "ffn2_w_in", "ffn2_w_out")}
    shared["pp"] = np.stack([_pack_params(inputs, l) for l in range(2)])
    shared["cst"] = _consts()
    tabs = {}
    for samp in (False, True):
        seg = SSEQ if samp else NTOK
        t = np.arange(NTOK) % seg
        rope = _rope_tables(t)
        mbt = np.zeros((128, NCH * NCH), np.float32)
        hft = np.ones((128, 2 * NT), np.float32)
        if samp:
            for a in range(NCH):
                for c in range(NCH):
                    if a != c:
                        mbt[:, a * NCH + c] = NEG
        for ti in range(NT):
            if (ti * T) % seg == 0:
                hft[:, 2 * ti] = 0.0
            if ((ti + 1) * T) % seg == 0:
                hft[:, 2 * ti + 1] = 0.0
        icnt = np.zeros((4, NTOK), np.float32)
        for gi, win in enumerate((2, 4, 8, 16)):
            lo = np.clip(t - win // 2, 0, seg)
            hi = np.clip(t - win // 2 + win, 0, seg)
            icnt[gi] = 1.0 / (hi - lo).astype(np.float32)
        tabs[samp] = dict(rope=rope, mb=mbt, hf=hft, icnt=icnt)
    in_maps = []
    for xc, samp in zip(core_x, core_is_sample):
        m = dict(shared)
        m.update(tabs[bool(samp)])
        m["x"] = f(xc)
        in_maps.append(m)
    res = run_bass_kernel_spmd(nc, in_maps, core_ids=list(range(len(in_maps))), trace=bool(os.environ.get('KTRACE')))
    return res


def kernel(**inputs):
    xp = np.asarray(inputs["x_prompt"], dtype=np.float32)
    xs = np.asarray(inputs["x_sample"], dtype=np.float32)
    Bp, Sp, _ = xp.shape
    Bs, Ss, _ = xs.shape
    per = Sp // Ss
    core_x = [xp[i] for i in range(Bp)] + [xs[i * per:(i + 1) * per].reshape(Sp, D) for i in range(Bs // per)]
    flags = [False] * Bp + [True] * (Bs // per)
    res = run(inputs, Sp, Ss, core_x, flags)
    ys = [np.asarray(r["y"], dtype=np.float32) for r in res.results]
    y_prompt = np.stack(ys[:Bp]).reshape(Bp, Sp, D)
    y_sample = np.concatenate(ys[Bp:], axis=0).reshape(Bs, Ss, D)
    return (y_prompt, y_sample)
```
